# Optimizing a Trainium2 kernel written in Bass

```python
import jax, jax.numpy as jnp
from jax import lax
import numpy as np

D_MODEL = 1024
BATCH = 8
SEQ = 4096
DEPTH = 1

D_POOL = D_MODEL // 2
POOL_WINDOWS = (2, 4, 8, 16)
POOL_GROUP = D_POOL // len(POOL_WINDOWS)
D_ATTN = D_MODEL - D_POOL
HEAD_DIM = 64
N_HEADS = D_ATTN // HEAD_DIM
ROT_DIM = HEAD_DIM // 4
ROPE_THETA = 500000.0
MOBA_BLOCK = 256
MOBA_TOPK = 3
Q_CHUNK = 32
D_IN = 2 * D_POOL + 4 * D_ATTN
ALPHA = (2.0 * DEPTH) ** 0.25
BETA = (8.0 * DEPTH) ** -0.25
LN_EPS = 1e-5

kernel_name = "hymba_pool_moba_deepnorm"


def layer_norm(x, gain, bias):
    xf = x.astype(jnp.float32)
    mu = jnp.mean(xf, axis=-1, keepdims=True)
    var = jnp.mean(jnp.square(xf - mu), axis=-1, keepdims=True)
    return ((xf - mu) * lax.rsqrt(var + LN_EPS) * gain + bias).astype(x.dtype)


def partial_rotary(x, positions):
    half = ROT_DIM // 2
    freqs = ROPE_THETA ** (-jnp.arange(half, dtype=jnp.float32) * 2.0 / ROT_DIM)
    ang = positions.astype(jnp.float32)[..., None] * freqs
    cos = jnp.cos(ang)[:, :, None, :]
    sin = jnp.sin(ang)[:, :, None, :]
    xr = x[..., :ROT_DIM].astype(jnp.float32)
    x1, x2 = xr[..., :half], xr[..., half:]
    rot = jnp.concatenate([x1 * cos - x2 * sin, x2 * cos + x1 * sin], axis=-1)
    return jnp.concatenate([rot.astype(x.dtype), x[..., ROT_DIM:]], axis=-1)


def multiscale_pool(u, pool_w, pool_scale):
    b, s, _ = u.shape
    uf = u.astype(jnp.float32)
    cs = jnp.cumsum(uf, axis=1)
    t = jnp.arange(s)
    diffs = []
    for g, w in enumerate(POOL_WINDOWS):
        sl = slice(g * POOL_GROUP, (g + 1) * POOL_GROUP)
        cg = cs[..., sl]
        prev = jnp.pad(cg, ((0, 0), (w, 0), (0, 0)))[:, :s]
        count = jnp.minimum(t + 1, w).astype(jnp.float32)[None, :, None]
        diffs.append((cg - prev) / count - uf[..., sl])
    d = jnp.stack(diffs, axis=2).astype(u.dtype)
    y = jnp.einsum('bsgc,gcd->bsgd', d, pool_w).reshape(b, s, D_POOL)
    return y * pool_scale


def moba_attention(q, k, v):
    b, s, h, d = q.shape
    nb = -(-s // MOBA_BLOCK)
    pad = nb * MOBA_BLOCK - s
    qh = q.transpose(0, 2, 1, 3)
    kh = jnp.pad(k.transpose(0, 2, 1, 3), ((0, 0), (0, 0), (0, pad), (0, 0)))
    vh = jnp.pad(v.transpose(0, 2, 1, 3), ((0, 0), (0, 0), (0, pad), (0, 0)))
    k_blocks = kh.reshape(b, h, nb, MOBA_BLOCK, d)
    v_blocks = vh.reshape(b, h, nb, MOBA_BLOCK, d)
    k_mean = jnp.mean(k_blocks.astype(jnp.float32), axis=3)
    n_sel = min(MOBA_TOPK, nb)
    scale = HEAD_DIM ** -0.5
    bi = jnp.arange(b)[:, None, None, None]
    hi = jnp.arange(h)[None, :, None, None]
    blk = jnp.arange(nb)
    kpos = jnp.arange(MOBA_BLOCK)
    qoff = jnp.arange(Q_CHUNK)

    def chunk(c):
        start = c * Q_CHUNK
        cur = start // MOBA_BLOCK
        qc = lax.dynamic_slice_in_dim(qh, start, Q_CHUNK, axis=2)
        qpos = start + qoff
        gate = jnp.einsum('bhqd,bhnd->bhqn', qc.astype(jnp.float32), k_mean)
        gate = jnp.where(blk < cur, gate, -jnp.inf)
        _, idx = lax.top_k(gate, n_sel)
        valid = idx < cur
        kg = k_blocks[bi, hi, idx]
        vg = v_blocks[bi, hi, idx]
        s_past = jnp.einsum('bhqd,bhqnkd->bhqnk', qc, kg).astype(jnp.float32) * scale
        s_past = jnp.where(valid[..., None], s_past, -jnp.inf)
        s_past = s_past.reshape(b, h, Q_CHUNK, n_sel * MOBA_BLOCK)
        k_own = lax.dynamic_slice_in_dim(kh, cur * MOBA_BLOCK, MOBA_BLOCK, axis=2)
        v_own = lax.dynamic_slice_in_dim(vh, cur * MOBA_BLOCK, MOBA_BLOCK, axis=2)
        s_own = jnp.einsum('bhqd,bhkd->bhqk', qc, k_own).astype(jnp.float32) * scale
        causal = (cur * MOBA_BLOCK + kpos)[None, :] <= qpos[:, None]
        s_own = jnp.where(causal, s_own, -jnp.inf)
        p = jax.nn.softmax(jnp.concatenate([s_past, s_own], axis=-1), axis=-1).astype(v.dtype)
        p_past = p[..., :n_sel * MOBA_BLOCK].reshape(b, h, Q_CHUNK, n_sel, MOBA_BLOCK)
        p_own = p[..., n_sel * MOBA_BLOCK:]
        return (jnp.einsum('bhqnk,bhqnkd->bhqd', p_past, vg)
                + jnp.einsum('bhqk,bhkd->bhqd', p_own, v_own))

    out = lax.map(chunk, jnp.arange(s // Q_CHUNK))
    return out.transpose(1, 0, 3, 2, 4).reshape(b, s, h * d)


def setup_inputs(seed: int = 0) -> dict:
    key = jax.random.key(seed)
    ks = jax.random.split(key, 8)
    x = jax.random.normal(ks[0], (BATCH, SEQ, D_MODEL), jnp.float32)
    positions = jnp.broadcast_to(jnp.arange(SEQ, dtype=jnp.int32), (BATCH, SEQ))
    w_in = jax.random.normal(ks[1], (DEPTH, D_MODEL, D_IN), jnp.float32) * D_MODEL ** -0.5
    pool_w = jax.random.normal(ks[2], (DEPTH, len(POOL_WINDOWS), POOL_GROUP, POOL_GROUP), jnp.float32) * POOL_GROUP ** -0.5
    pool_scale = 1.0 + 0.02 * jax.random.normal(ks[3], (DEPTH, D_POOL), jnp.float32)
    w_out = jax.random.normal(ks[4], (DEPTH, D_MODEL, D_MODEL), jnp.float32) * (D_MODEL ** -0.5) * BETA
    ln_gain = 1.0 + 0.02 * jax.random.normal(ks[5], (DEPTH, D_MODEL), jnp.float32)
    ln_bias = 0.02 * jax.random.normal(ks[6], (DEPTH, D_MODEL), jnp.float32)
    return {'x': x, 'positions': positions, 'w_in': w_in, 'pool_w': pool_w,
            'pool_scale': pool_scale, 'w_out': w_out, 'ln_gain': ln_gain, 'ln_bias': ln_bias}


def reference(x, positions, w_in, pool_w, pool_scale, w_out, ln_gain, ln_bias):
    b, s, _ = x.shape
    h = x
    cuts = [D_POOL, 2 * D_POOL, 2 * D_POOL + D_ATTN, 2 * D_POOL + 2 * D_ATTN, 2 * D_POOL + 3 * D_ATTN]
    for layer in range(DEPTH):
        proj = jnp.einsum('bsd,de->bse', h, w_in[layer])
        u_pool, g_pool, q, k, v, g_attn = jnp.split(proj, cuts, axis=-1)
        y_pool = multiscale_pool(u_pool, pool_w[layer], pool_scale[layer]) * jax.nn.silu(g_pool)
        q = partial_rotary(q.reshape(b, s, N_HEADS, HEAD_DIM), positions)
        k = partial_rotary(k.reshape(b, s, N_HEADS, HEAD_DIM), positions)
        v = v.reshape(b, s, N_HEADS, HEAD_DIM)
        y_attn = moba_attention(q, k, v) * jax.nn.silu(g_attn)
        mix = jnp.concatenate([y_pool, y_attn], axis=-1)
        out = jnp.einsum('bse,ed->bsd', mix, w_out[layer])
        h = layer_norm(ALPHA * h + out, ln_gain[layer], ln_bias[layer])
    return h
```

```python
import math
from contextlib import ExitStack

import numpy as np
import concourse.bass as bass
import concourse.mybir as mybir
from concourse.bass_utils import run_bass_kernel_spmd

F32 = mybir.dt.float32
BF16 = mybir.dt.bfloat16
I32 = mybir.dt.int32
AF = mybir.ActivationFunctionType
ALU = mybir.AluOpType
AX = mybir.AxisListType

S = 4096
D = 1024
NT = S // 128
NG = S // 512
NEG = -30000.0
ALPHA = 2.0 ** 0.25
LN_EPS = 1e-5
DEBUG = False


class Prog:
    def __init__(self):
        self.ops = []

    def op(self, eng, fn, r=(), w=(), dma=None):
        self.ops.append(dict(eng=eng, fn=fn, r=tuple(r), w=tuple(w), dma=dma,
                             deps=set(), sig=dma is not None))
        return len(self.ops) - 1

    def analyze(self):
        last_w = {}
        readers = {}
        for i, o in enumerate(self.ops):
            deps = set()
            for t in o["r"]:
                if t in last_w:
                    deps.add(last_w[t])
            for t in o["w"]:
                if t in last_w:
                    deps.add(last_w[t])
                for j in readers.get(t, ()):
                    deps.add(j)
            deps.discard(i)
            for t in o["w"]:
                last_w[t] = i
                readers[t] = []
            for t in o["r"]:
                if t not in o["w"]:
                    readers.setdefault(t, []).append(i)
            keep = set()
            for j in deps:
                p = self.ops[j]
                if p["dma"] is None and p["eng"] == "pe" and o["eng"] == "pe" and o["dma"] is None:
                    continue
                keep.add(j)
            o["deps"] = keep
            for j in keep:
                self.ops[j]["sig"] = True

    def emit(self, sems):
        self.analyze()
        cnt = {}
        for o in self.ops:
            key = o["dma"] if o["dma"] is not None else o["eng"]
            o["key"] = key
            if o["sig"]:
                cnt[key] = cnt.get(key, 0) + 1
                o["count"] = cnt[key] * (16 if o["dma"] is not None else 1)
        streams = {}
        for i, o in enumerate(self.ops):
            streams.setdefault(o["eng"], []).append(i)
        self.total = dict(cnt)

        def run_stream(engname, eng):
            waited = {}
            for i in streams.get(engname, []):
                o = self.ops[i]
                need = {}
                for j in o["deps"]:
                    p = self.ops[j]
                    k = p["key"]
                    need[k] = max(need.get(k, 0), p["count"])
                for k, v in need.items():
                    if waited.get(k, 0) < v:
                        eng.wait_ge(sems[k], v)
                        waited[k] = v
                ins = o["fn"](eng)
                if o["sig"]:
                    ins.then_inc(sems[o["key"]], 16 if o["dma"] is not None else 1)

        return run_stream


def _mm(P, out, lhsT, rhs, start, stop, r, w, tp=None):
    def fn(e):
        if tp is None:
            return e.matmul(out, lhsT=lhsT, rhs=rhs, start=start, stop=stop)
        return e.matmul(out, lhsT=lhsT, rhs=rhs, start=start, stop=stop, tile_position=tp)
    P.op("pe", fn, r=r, w=w)


def _tr(P, out, in_, ident, r, w):
    P.op("pe", lambda e: e.transpose(out=out, in_=in_, identity=ident), r=r, w=w)


def _act(P, out, in_, func, r, w, scale=None):
    def fn(e):
        if scale is None:
            return e.activation(out=out, in_=in_, func=func)
        return e.activation(out=out, in_=in_, func=func, scale=scale)
    P.op("act", fn, r=r, w=w)


def _copy(P, eng, out, in_, r, w):
    if eng == "act":
        _act(P, out, in_, AF.Copy, r, w)
    else:
        P.op(eng, lambda e: e.tensor_copy(out=out, in_=in_), r=r, w=w)


def _tt(P, eng, out, in0, in1, op, r, w):
    P.op(eng, lambda e: e.tensor_tensor(out=out, in0=in0, in1=in1, op=op), r=r, w=w)


def _ts(P, eng, out, in0, s1, op0, r, w, s2=None, op1=None):
    def fn(e):
        if op1 is None:
            return e.tensor_scalar(out=out, in0=in0, scalar1=s1, scalar2=None, op0=op0)
        return e.tensor_scalar(out=out, in0=in0, scalar1=s1, scalar2=s2, op0=op0, op1=op1)
    P.op(eng, fn, r=r, w=w)


def _stt(P, out, in0, scalar, in1, op0, op1, r, w):
    P.op("dve", lambda e: e.scalar_tensor_tensor(out=out, in0=in0, scalar=scalar, in1=in1, op0=op0, op1=op1),
         r=r, w=w)


def _memset(P, eng, ap, val, r, w):
    P.op(eng, lambda e: e.memset(ap, val), r=r, w=w)


def _dma(P, eng, out, in_, stream, r, w):
    key = ("L:" + w[0]) if not r else ("S:" + r[0])
    P.op(eng, lambda e: e.dma_start(out=out, in_=in_), r=r, w=w, dma=key)


def _run_block(nc, P, sem_names, es, tag):
    sems = {n: es.enter_context(nc.semaphore(tag + "_" + n.replace(":", "_"))) for n in sem_names}
    run = P.emit(sems)
    with nc.Block() as block:
        @block.sync
        def _(e):
            run("sp", e)
            for k, v in P.total.items():
                if k not in ("pe", "act", "dve", "pool"):
                    e.wait_ge(sems[k], 16 * v)

        @block.tensor
        def _(e):
            run("pe", e)

        @block.scalar
        def _(e):
            run("act", e)

        @block.vector
        def _(e):
            run("dve", e)

        @block.gpsimd
        def _(e):
            run("pool", e)


def _sem_names(P):
    names = {"pe", "act", "dve", "pool"}
    for o in P.ops:
        if o["dma"] is not None:
            names.add(o["dma"])
    return sorted(names)


def build_nc():
    nc = bass.Bass("TRN2", target_bir_lowering=False)
    xT_d = nc.dram_tensor("xT", [D, S], F32, kind="ExternalInput")
    x_d = nc.dram_tensor("x", [S, D], F32, kind="ExternalInput")
    pos_d = nc.dram_tensor("pos", [128, NT], I32, kind="ExternalInput")
    win_d = nc.dram_tensor("w_in", [D, 3072], F32, kind="ExternalInput")
    poolw_d = nc.dram_tensor("pool_w", [4, 128, 128], F32, kind="ExternalInput")
    pscale_d = nc.dram_tensor("pool_scale", [128, 4], F32, kind="ExternalInput")
    wout_d = nc.dram_tensor("w_out", [D, D], F32, kind="ExternalInput")
    gain_d = nc.dram_tensor("ln_gain", [1, D], F32, kind="ExternalInput")
    lbias_d = nc.dram_tensor("ln_bias", [1, D], F32, kind="ExternalInput")
    out_d = nc.dram_tensor("out", [S, D], F32, kind="ExternalOutput")
    skind = "ExternalOutput" if DEBUG else "Internal"
    qsp_d = nc.dram_tensor("q_sp", [8, 80, S], BF16, kind=skind)
    gsp_d = nc.dram_tensor("g_sp", [512, S], F32, kind=skind)
    mpsp_d = nc.dram_tensor("mp_sp", [512, S], BF16, kind=skind)
    if DEBUG:
        kdbg_d = nc.dram_tensor("k_dbg", [80, 8 * S], BF16, kind="ExternalOutput")
        vdbg_d = nc.dram_tensor("v_dbg", [128, NT * 8 * 96], BF16, kind="ExternalOutput")

    def dap(t, off, pat):
        return bass.AP(t, off, pat)

    with ExitStack() as g_es:
        def gsb(name, shape, dt):
            return g_es.enter_context(nc.sbuf_tensor(name, shape, dt))
        kaug = gsb("kaug", [128, 8, S], BF16)
        vaug = gsb("vaug", [128, NT, 8, 96], BF16)
        wbuf = gsb("wbuf", [128, 8, 1536], BF16)
        xg = gsb("xg", [128, 4 * D], F32)
        xTbf_g = [xg[:, j * 2048:(j + 1) * 2048].bitcast(BF16).rearrange("p (k n) -> p k n", k=8) for j in range(2)]
        ident = gsb("ident", [128, 128], BF16)
        tribias = gsb("tribias", [128, 128], BF16)
        ones_bf = gsb("ones_bf", [128, 128], BF16)
        zeros_bf = gsb("zeros_bf", [128, 128], BF16)
        cosT = gsb("cosT", [128, NT, 8], F32)
        sinT = gsb("sinT", [128, NT, 8], F32)

        with ExitStack() as es:
            def sb(name, shape, dt):
                return es.enter_context(nc.sbuf_tensor(name, shape, dt))

            def ps(name, shape, dt=F32):
                return es.enter_context(nc.psum_tensor(name, shape, dt))
            P = Prog()
            w_fm = wbuf
            xTbf = xTbf_g
            poolw_bf = sb("poolw_bf", [128, 4, 128], BF16)
            pscale = sb("pscale", [128, 4], F32)
            ubuf = sb("ubuf", [128, 4, 528], F32)
            Sa = sb("Sa", [128, 528], F32)
            Sb = sb("Sb", [128, 528], F32)
            d_bf = sb("d_bf", [128, 4, 512], BF16)
            sg = [sb(f"sg{i}", [128, 512], F32) for i in range(2)]
            mp_stage = [sb(f"mp_stage{i}", [128, 4, 512], BF16) for i in range(2)]
            sga_stage = [sb(f"sga_stage{i}", [128, 512], F32) for i in range(2)]
            rcw = sb("rcw", [128, 16], F32)
            fix = sb("fix", [128, 16], F32)
            posi = sb("posi", [128, NT], I32)
            posf = sb("posf", [128, NT], F32)
            ang = sb("ang", [128, NT * 8], F32)
            angc = sb("angc", [128, NT * 8], F32)
            yy = sb("yy", [128, NT * 8], F32)
            ki = sb("ki", [128, NT * 8], I32)
            kf = sb("kf", [128, NT * 8], F32)
            mk = sb("mk", [128, NT * 8], F32)
            pj = [ps(f"pjA{i}", [128, 512]) for i in range(4)]
            yp = [ps(f"ypA{i}", [128, 512]) for i in range(2)]

            def xload_a1(g):
                _dma(P, "pool", xTbf[g % 2], dap(xT_d, g * 512, [[S, 128], [128 * S, 8], [1, 512]]), "xld",
                     [], [f"xTA{g % 2}"])
            xload_a1(0)
            for k0, k1 in ((0, 4), (4, 8)):
                nk = k1 - k0
                _dma(P, "pool", w_fm[:, k0:k1, 0:1024], dap(win_d, k0 * 128 * 3072, [[3072, 128], [128 * 3072, nk], [1, 1024]]),
                     "wld", [], [f"w_fm{kc}a" for kc in range(k0, k1)])
            for k0, k1 in ((0, 4), (4, 8)):
                nk = k1 - k0
                _dma(P, "pool", w_fm[:, k0:k1, 1024:1536], dap(win_d, k0 * 128 * 3072 + 2560, [[3072, 128], [128 * 3072, nk], [1, 512]]),
                     "wld", [], [f"w_fm{kc}b" for kc in range(k0, k1)])
            _dma(P, "pool", poolw_bf[:], dap(poolw_d, 0, [[128, 128], [128 * 128, 4], [1, 128]]), "wld", [], ["poolw"])
            _dma(P, "sp", pscale[:], pscale_d.ap(), "misc", [], ["pscale"])
            _dma(P, "sp", posi[:], pos_d.ap(), "misc", [], ["posi"])

            _memset(P, "pool", ones_bf[:], 1.0, [], ["ones"])
            _memset(P, "pool", zeros_bf[:], 0.0, [], ["zeros"])
            P.op("pool", lambda e: e.affine_select(out=ident[:], in_=ones_bf[:], pattern=[[-1, 128]],
                                                   compare_op=ALU.is_equal, fill=0.0, base=0, channel_multiplier=1),
                 r=["ones"], w=["ident"])
            P.op("pool", lambda e: e.affine_select(out=tribias[:], in_=zeros_bf[:], pattern=[[1, 128]],
                                                   compare_op=ALU.is_ge, fill=NEG, base=0, channel_multiplier=-1),
                 r=["zeros"], w=["tribias"])
            for t in range(15):
                _memset(P, "pool", rcw[:, t:t + 1], 1.0 / (t + 1), ["rcw"], ["rcw"])
            _memset(P, "pool", rcw[:, 15:16], 1.0 / 16, ["rcw"], ["rcw"])
            freqs = (np.float32(500000.0) ** (-(np.arange(8, dtype=np.float32) * np.float32(2.0)) / np.float32(16.0))).astype(np.float32)
            _copy(P, "dve", posf[:], posi[:], ["posi"], ["posf"])
            ang3 = ang[:].rearrange("p (t j) -> p t j", j=8)
            for j in range(8):
                _ts(P, "dve", ang3[:, :, j], posf[:], float(freqs[j]), ALU.mult, ["posf", "ang"], ["ang"])
            _ts(P, "dve", angc[:], ang[:], math.pi / 2, ALU.add, ["ang"], ["angc"])
            C1 = 6.28125
            C2 = 2 * math.pi - 6.28125

            def sin_of(src, srctok, dst):
                _ts(P, "dve", yy[:], src[:], 1.0 / (2 * math.pi), ALU.mult, [srctok, "yy"], ["yy"])
                _copy(P, "dve", ki[:], yy[:], ["yy", "ki"], ["ki"])
                _copy(P, "dve", kf[:], ki[:], ["ki", "kf"], ["kf"])
                _stt(P, src[:], kf[:], -C1, src[:], ALU.mult, ALU.add, ["kf", srctok], [srctok])
                _stt(P, src[:], kf[:], -C2, src[:], ALU.mult, ALU.add, ["kf", srctok], [srctok])
                _ts(P, "dve", mk[:], src[:], math.pi, ALU.is_gt, [srctok, "mk"], ["mk"])
                _stt(P, src[:], mk[:], -2 * math.pi, src[:], ALU.mult, ALU.add, ["mk", srctok], [srctok])
                _ts(P, "dve", src[:], src[:], -math.pi, ALU.max, [srctok], [srctok], s2=math.pi, op1=ALU.min)
                _act(P, dst[:].rearrange("p t j -> p (t j)"), src[:], AF.Sin, [srctok], ["rot_tab"])
            sin_of(ang, "ang", sinT)
            sin_of(angc, "angc", cosT)

            chunk_seq = [("u", 0), ("gp", 0), ("u", 1), ("gp", 1), ("u", 2), ("gp", 2), ("u", 3), ("gp", 3),
                         ("ga", 0), ("ga", 1), ("ga", 2), ("ga", 3)]
            nbank = 0

            for g in range(NG):
                xb = g % 2
                if g + 1 < NG:
                    xload_a1(g + 1)
                else:
                    _dma(P, "pool", xTbf[0], dap(xT_d, 0, [[S, 128], [128 * S, 8], [1, 512]]), "xld", [], ["xTA0"])
                pend_pool_mm = []

                def pool_mm(gi, g=g, xb=xb):
                    _mm(P, yp[gi % 2][:], poolw_bf[:, gi, :], d_bf[:, gi, :], True, True,
                        ["poolw", f"d{gi}"], [f"yp{gi % 2}"])
                    _stt(P, mp_stage[xb][:, gi, :], yp[gi % 2][:], pscale[:, gi:gi + 1], sg[gi % 2][:],
                         ALU.mult, ALU.mult, [f"yp{gi % 2}", "pscale", f"sg{gi % 2}"], [f"mp{xb}_{gi}"])

                for kind, idx in chunk_seq:
                    bank = pj[nbank % 4]
                    btok = f"pj{nbank % 4}"
                    nbank += 1
                    col0 = {"u": idx * 128, "gp": 512 + idx * 128, "ga": 1024 + idx * 128}[kind]
                    for kc in range(8):
                        _mm(P, bank[:], w_fm[:, kc, col0:col0 + 128], xTbf[xb][:, kc, :], kc == 0, kc == 7,
                            [f"w_fm{kc}b" if kind == "ga" else f"w_fm{kc}a", f"xTA{xb}"], [btok])
                    if kind == "u":
                        gi = idx
                        w = 2 << gi
                        ut = f"ubuf{gi}"
                        if len(pend_pool_mm) >= 2:
                            pool_mm(pend_pool_mm.pop(0))
                        if g == 0:
                            _memset(P, "pool", ubuf[:, gi, 0:16], 0.0, [ut], [ut])
                        else:
                            _copy(P, "pool", ubuf[:, gi, 0:16], ubuf[:, gi, 512:528], [ut], [ut])
                        _copy(P, "act", ubuf[:, gi, 16:528], bank[:], [btok, ut], [ut])
                        _tt(P, "pool", Sa[:, 1:528], ubuf[:, gi, 1:528], ubuf[:, gi, 0:527], ALU.add, [ut, "Sa"], ["Sa"])
                        cur, curt = Sa, "Sa"
                        if w >= 4:
                            _tt(P, "pool", Sb[:, 3:528], Sa[:, 3:528], Sa[:, 1:526], ALU.add, ["Sa", "Sb"], ["Sb"])
                            cur, curt = Sb, "Sb"
                        if w >= 8:
                            _tt(P, "pool", Sa[:, 7:528], Sb[:, 7:528], Sb[:, 3:524], ALU.add, ["Sb", "Sa"], ["Sa"])
                            cur, curt = Sa, "Sa"
                        if w >= 16:
                            _tt(P, "pool", Sb[:, 15:528], Sa[:, 15:528], Sa[:, 7:520], ALU.add, ["Sa", "Sb"], ["Sb"])
                            cur, curt = Sb, "Sb"
                        _stt(P, d_bf[:, gi, :], cur[:, 16:528], 1.0 / w, ubuf[:, gi, 16:528], ALU.mult, ALU.subtract,
                             [curt, ut, f"d{gi}"], [f"d{gi}"])
                        if g == 0:
                            _tt(P, "dve", fix[:, 0:w - 1], cur[:, 16:16 + w - 1], rcw[:, 0:w - 1], ALU.mult,
                                [curt, "rcw", "fix"], ["fix"])
                            _tt(P, "dve", d_bf[:, gi, 0:w - 1], fix[:, 0:w - 1], ubuf[:, gi, 16:16 + w - 1], ALU.subtract,
                                ["fix", ut, f"d{gi}"], [f"d{gi}"])
                        pend_pool_mm.append(gi)
                    elif kind == "gp":
                        gi = idx
                        _act(P, sg[gi % 2][:], bank[:], AF.Silu, [btok, f"sg{gi % 2}"], [f"sg{gi % 2}"])
                        if g == NG - 1 and gi == 3:
                            _dma(P, "pool", wbuf[:, :, 0:1024], dap(win_d, 1024, [[3072, 128], [128 * 3072, 8], [1, 1024]]),
                                 "wld", [], [f"w_fm{kc}a" for kc in range(8)])
                    else:
                        j = idx
                        st = sga_stage[j % 2]
                        _act(P, st[:], bank[:], AF.Silu, [btok, f"sga{j % 2}"], [f"sga{j % 2}"])
                        _dma(P, "sp", dap(gsp_d, (j * 128) * S + g * 512, [[S, 128], [1, 512]]), st[:], "spw",
                             [f"sga{j % 2}"], [f"gsp{g}"])
                        if pend_pool_mm and j in (0, 2):
                            pool_mm(pend_pool_mm.pop(0))
                while pend_pool_mm:
                    pool_mm(pend_pool_mm.pop(0))
                _dma(P, "sp", dap(mpsp_d, g * 512, [[S, 128], [128 * S, 4], [1, 512]]), mp_stage[xb][:], "spw",
                     [f"mp{xb}_{i}" for i in range(4)], [f"mpsp{g}"])
            _dma(P, "pool", wbuf[:, :, 1024:1536], dap(win_d, 2048, [[3072, 128], [128 * 3072, 8], [1, 512]]),
                 "wld", [], [f"w_fm{kc}b" for kc in range(8)])
            _run_block(nc, P, _sem_names(P), g_es, "a1")

        with ExitStack() as es:
            def sb(name, shape, dt):
                return es.enter_context(nc.sbuf_tensor(name, shape, dt))

            def ps(name, shape, dt=F32):
                return es.enter_context(nc.psum_tensor(name, shape, dt))
            P = Prog()
            w_tm = wbuf
            xTbf = xTbf_g
            NQK = 4
            qk_tok = [sb(f"qk_tok{i}", [128, 16, 80], BF16) for i in range(NQK)]
            rt = [sb(f"rt{i}", [128, 16, 8], F32) for i in range(4)]
            qTg = [sb(f"qTg{i}", [128, 8, 128], BF16) for i in range(2)]
            gm = sb("gm", [128, 8, 16], F32)
            top8 = sb("top8", [128, 8, 8], F32)
            sel = sb("sel", [128, 8, 16], F32)
            ksum = sb("ksum", [128, 8], F32)
            kmT = sb("kmT", [128, 8, 16], BF16)
            qstage = [sb(f"qstage{i}", [128, 8, 512], BF16) for i in range(2)]
            qb = ps("qbank", [128, 512])
            kb = ps("kbank", [128, 512])
            vb = ps("vbank", [128, 512])
            NTR = 4
            tr = [ps(f"trb{i}", [128, 1024], BF16) for i in range(NTR)]
            gt = ps("gtbank", [128, 512])

            _memset(P, "dve", gm[:], -1.0e30, [], ["gm"])
            for t8 in range(0, NT, 8):
                _memset(P, "pool", vaug[:, t8:t8 + 8, :, 64:96], 1.0, [], ["vones"])
            ntr_box = [0]
            trk_of = {}
            trq_of = {}

            def next_tr():
                i = ntr_box[0] % NTR
                ntr_box[0] += 1
                return tr[i], f"tr{i}"

            def xload_a2(g):
                _dma(P, "pool", xTbf[g % 2], dap(xT_d, g * 512, [[S, 128], [128 * S, 8], [1, 512]]), "xld",
                     [], [f"xTB{g % 2}"])

            def S12(t):
                g, tt = divmod(t, 4)
                xb = g % 2
                cur = t // 2
                qk = qk_tok[t % NQK]
                qkt = f"qk{t % NQK}"
                for ci, (bank, btok) in enumerate(((qb, "qb"), (kb, "kb"), (vb, "vb"))):
                    for kc in range(8):
                        _mm(P, bank[:], xTbf[xb][:, kc, tt * 128:(tt + 1) * 128], w_tm[:, kc, ci * 512:(ci + 1) * 512],
                            kc == 0, kc == 7, [f"xTB{xb}", f"w_tm{kc}"], [btok])
                cosb = cosT[:, t:t + 1, :].to_broadcast([128, 8, 8])
                sinb = sinT[:, t:t + 1, :].to_broadcast([128, 8, 8])
                for bank, btok, h0 in ((qb, "qb", 0), (kb, "kb", 8)):
                    X = bank[:].rearrange("p (h d) -> p h d", d=64)
                    x1 = X[:, :, 0:8]
                    x2 = X[:, :, 8:16]
                    hs = slice(h0, h0 + 8)
                    _tt(P, "dve", rt[0][:, hs, :], x1, cosb, ALU.mult, [btok, "rot_tab", "rt0"], ["rt0"])
                    _tt(P, "dve", rt[1][:, hs, :], x2, sinb, ALU.mult, [btok, "rot_tab", "rt1"], ["rt1"])
                    _tt(P, "dve", rt[2][:, hs, :], x2, cosb, ALU.mult, [btok, "rot_tab", "rt2"], ["rt2"])
                    _tt(P, "dve", rt[3][:, hs, :], x1, sinb, ALU.mult, [btok, "rot_tab", "rt3"], ["rt3"])
                    _tt(P, "dve", qk[:, hs, 0:8], rt[0][:, hs, :], rt[1][:, hs, :], ALU.subtract, ["rt0", "rt1", qkt], [qkt])
                    _tt(P, "dve", qk[:, hs, 8:16], rt[2][:, hs, :], rt[3][:, hs, :], ALU.add, ["rt2", "rt3", qkt], [qkt])
                    _copy(P, "act", qk[:, hs, 16:64], X[:, :, 16:64], [btok, qkt], [qkt])
                _copy(P, "act", vaug[:, t, :, 0:64], vb[:].rearrange("p (h d) -> p h d", d=64), ["vb"], [f"v{t}"])
                _memset(P, "pool", qk[:, 8:16, 64:80], 0.0, [qkt], [qkt])
                _memset(P, "pool", qk[:, 8:16, 64 + cur:65 + cur], 1.0, [qkt], [qkt])
                if cur == 0:
                    _memset(P, "pool", qk[:, 0:8, 64:80], 0.0, [qkt], [qkt])

            def S34(t):
                cur = t // 2
                qk = qk_tok[t % NQK]
                qkt = f"qk{t % NQK}"
                trk, trkt = next_tr()
                for h in range(8):
                    _tr(P, trk[0:80, h * 128:(h + 1) * 128], qk[:, 8 + h, 0:80], ident[:], [qkt, "ident"], [trkt])
                if cur > 0:
                    trq, trqt = next_tr()
                    for h in range(8):
                        _tr(P, trq[0:64, h * 128:(h + 1) * 128], qk[:, h, 0:64], ident[:], [qkt, "ident"], [trqt])
                _copy(P, "act", kaug[0:80, :, t * 128:(t + 1) * 128], trk[0:80, :].rearrange("p (h n) -> p h n", h=8),
                      [trkt], [f"kaug{t}"])
                if cur > 0:
                    _copy(P, "dve", qTg[t % 2][0:64, :, :], trq[0:64, :].rearrange("p (h n) -> p h n", h=8),
                          [trqt, f"qTg{t % 2}"], [f"qTg{t % 2}"])
                for h in range(8):
                    _mm(P, gt[0:64, 128 + h:129 + h], qk[:, 8 + h, 0:64], ones_bf[:, 0:1], True, True, [qkt], ["gt"])
                if t % 2 == 0:
                    _copy(P, "act", ksum[0:64, :], gt[0:64, 128:136], ["gt", "ksum"], ["ksum"])
                else:
                    _tt(P, "dve", kmT[0:64, :, cur], ksum[0:64, :], gt[0:64, 128:136], ALU.add, ["ksum", "gt"], [f"km{cur}"])

            def S56(t):
                cur = t // 2
                if cur == 0:
                    return
                qk = qk_tok[t % NQK]
                qkt = f"qk{t % NQK}"
                for h in range(8):
                    _mm(P, gt[:, h * 16:h * 16 + cur], qTg[t % 2][0:64, h, :], kmT[0:64, h, 0:cur], True, True,
                        [f"qTg{t % 2}"] + [f"km{n}" for n in range(cur)], ["gt"])
                gt3 = gt[:, 0:128].rearrange("p (h n) -> p h n", n=16)
                _copy(P, "dve", gm[:, :, 0:cur], gt3[:, :, 0:cur], ["gt", "gm"], ["gm"])
                for h in range(8):
                    P.op("dve", (lambda h: lambda e: e.max(out=top8[:, h, :], in_=gm[:, h, :]))(h),
                         r=["gm", "top8"], w=["top8"])
                _tt(P, "dve", sel[:], gm[:], top8[:, :, 2:3].to_broadcast([128, 8, 16]), ALU.is_ge,
                    ["gm", "top8", "sel"], ["sel"])
                _ts(P, "dve", qk[:, 0:8, 64:80], sel[:], -1.0, ALU.add, ["sel", qkt], [qkt], s2=-NEG, op1=ALU.mult)
                _memset(P, "pool", qk[:, 0:8, 64 + cur:65 + cur], 0.0, [qkt], [qkt])

            def S78(t):
                g, tt = divmod(t, 4)
                xb = g % 2
                qk = qk_tok[t % NQK]
                qkt = f"qk{t % NQK}"
                trf, trft = next_tr()
                for h in range(8):
                    _tr(P, trf[0:80, h * 128:(h + 1) * 128], qk[:, h, 0:80], ident[:], [qkt, "ident"], [trft])
                _copy(P, "act", qstage[xb][0:80, :, tt * 128:(tt + 1) * 128],
                      trf[0:80, :].rearrange("p (h n) -> p h n", h=8), [trft, f"qst{xb}"], [f"qst{xb}"])
                if tt == 3:
                    _dma(P, "sp", dap(qsp_d, g * 512, [[S, 80], [80 * S, 8], [1, 512]]), qstage[xb][0:80, :, :], "spw",
                         [f"qst{xb}"], [f"qsp{g}"])

            for i in range(NT + 3):
                if i % 4 == 0 and i // 4 + 1 < NG and i < NT:
                    xload_a2(i // 4 + 1)
                if 0 <= i - 3 < NT:
                    S78(i - 3)
                if 0 <= i - 2 < NT:
                    S56(i - 2)
                if 0 <= i - 1 < NT:
                    S34(i - 1)
                if i < NT:
                    S12(i)
            if DEBUG:
                _dma(P, "sp", kdbg_d.ap(), kaug[0:80, :, :].rearrange("p h n -> p (h n)"), "spw",
                     [f"kaug{t}" for t in range(NT)], ["kdbg"])
                _dma(P, "sp", vdbg_d.ap(), vaug[:].rearrange("p t h n -> p (t h n)"), "spw",
                     [f"v{t}" for t in range(NT)], ["vdbg"])
            _run_block(nc, P, _sem_names(P), g_es, "a2")

        with ExitStack() as es:
            def sb(name, shape, dt):
                return es.enter_context(nc.sbuf_tensor(name, shape, dt))

            def ps(name, shape, dt=F32):
                return es.enter_context(nc.psum_tensor(name, shape, dt))
            P = Prog()
            wout_bf = wbuf[:, :, 0:D]
            gain_b = sb("gain_b", [128, D], F32)
            bias_b = sb("bias_b", [128, D], F32)
            qaug = [sb(f"qaug{i}", [128, 8, 512], BF16) for i in range(2)]
            mixT = [sb(f"mixT{i}", [128, 8, 512], BF16) for i in range(2)]
            sga = [sb(f"sga{i}", [128, 512], F32) for i in range(2)]
            PT = [sb(f"PT{i}", [128, 2, 512], BF16) for i in range(3)]
            ontok = sb("ontok", [128, 4, 128], BF16)
            rdn = sb("rdn", [128, 8], F32)
            NXB = 4
            xtok = [xg[:, i * D:(i + 1) * D] for i in range(NXB)]
            zt = xtok
            bst = sb("bst", [128, 12], F32)
            mv = [sb(f"mv{i}", [128, 4], F32) for i in range(NXB)]
            scp = [ps(f"scp{i}", [128, 2, 512]) for i in range(2)]
            accT = [ps(f"accT{i}", [128, 512]) for i in range(2)]
            opb1 = ps("opb1", [128, 512])
            trp = ps("trp", [128, 1024], BF16)

            _dma(P, "pool", wout_bf, dap(wout_d, 0, [[D, 128], [128 * D, 8], [1, D]]), "wld", [], ["wout"])
            _dma(P, "sp", gain_b[:], dap(gain_d, 0, [[0, 128], [1, D]]), "misc", [], ["gain"])
            _dma(P, "sp", bias_b[:], dap(lbias_d, 0, [[0, 128], [1, D]]), "misc", [], ["lbias"])

            def load_qs(g):
                gb = g % 2
                _dma(P, "sp", qaug[gb][0:80, :, :], dap(qsp_d, g * 512, [[S, 80], [80 * S, 8], [1, 512]]), "qld",
                     [], [f"qaug{gb}"])

            def load_sga(g, c):
                k = (4 * g + c) % 2
                _dma(P, "sp", sga[k][:], dap(gsp_d, (c * 128) * S + g * 512, [[S, 128], [1, 512]]), "qld",
                     [], [f"sga{k}"])

            def load_mixp(g):
                gb = g % 2
                _dma(P, "sp", mixT[gb][:, 0:4, :], dap(mpsp_d, g * 512, [[S, 128], [128 * S, 4], [1, 512]]), "qld",
                     [], [f"mixp{gb}"])

            def load_x(t):
                _dma(P, "sp", xtok[t % NXB][:], dap(x_d, t * 128 * D, [[D, 128], [1, D]]), "xres", [], [f"xtok{t % NXB}"])

            def QK(it):
                g, c, kt, n = it
                gb = g % 2
                j = kt - 4 * g
                c0 = 0 if j < 0 else 128 * j
                sc = scp[n % 2]
                sct = f"scp{n % 2}"
                for hi, h in enumerate((2 * c, 2 * c + 1)):
                    _mm(P, sc[:, hi, c0:512], kaug[0:80, h, kt * 128:(kt + 1) * 128], qaug[gb][0:80, h, c0:512],
                        True, j < 0, [f"qaug{gb}"], [sct])
                    if j >= 0:
                        _mm(P, sc[:, hi, c0:c0 + 128], ident[:], tribias[:], False, True, [], [sct])
                pt = PT[n % 3]
                ptt = f"PT{n % 3}"
                _act(P, pt[:, :, c0:512], sc[:, :, c0:512], AF.Exp, [sct, ptt], [ptt], scale=0.125)

            def PV(it):
                g, c, kt, n = it
                gb = g % 2
                j = kt - 4 * g
                s0 = 0 if j < 0 else j
                A, B_ = 2 * c, 2 * c + 1
                pt = PT[n % 3]
                ptt = f"PT{n % 3}"
                first = kt == 0
                last = kt == 4 * g + 3
                for hi, h in enumerate((A, B_)):
                    for sq in range(s0, 4):
                        _mm(P, accT[hi][:, sq * 128:sq * 128 + 65], pt[:, hi, sq * 128:(sq + 1) * 128], vaug[:, kt, h, 0:65],
                            first and sq == 0, last and sq == 3, [ptt], [f"accT{hi}"])
                if last:
                    k = (4 * g + c) % 2
                    XA = accT[0][:].rearrange("p (s c) -> p s c", c=128)
                    XB = accT[1][:].rearrange("p (s c) -> p s c", c=128)
                    P.op("dve", lambda e: e.reciprocal(out=rdn[:, 0:4], in_=XA[:, :, 64]), r=["accT0", "rdn"], w=["rdn"])
                    P.op("dve", lambda e: e.reciprocal(out=rdn[:, 4:8], in_=XB[:, :, 64]), r=["accT1", "rdn"], w=["rdn"])
                    rA = rdn[:, 0:4].rearrange("p (s o) -> p s o", o=1).to_broadcast([128, 4, 64])
                    rB = rdn[:, 4:8].rearrange("p (s o) -> p s o", o=1).to_broadcast([128, 4, 64])
                    _tt(P, "dve", ontok[:, :, 0:64], XA[:, :, 0:64], rA, ALU.mult, ["accT0", "rdn", "ontok"], ["ontok"])
                    _tt(P, "dve", ontok[:, :, 64:128], XB[:, :, 0:64], rB, ALU.mult, ["accT1", "rdn", "ontok"], ["ontok"])

                    def back_to_feature_major(g=g, c=c, gb=gb, k=k):
                        for sq in range(4):
                            _tr(P, trp[:, sq * 128:(sq + 1) * 128], ontok[:, sq, :], ident[:], ["ontok", "ident"], ["trp"])
                        _tt(P, "dve", mixT[gb][:, 4 + c, :], trp[:, 0:512], sga[k][:], ALU.mult,
                            ["trp", f"sga{k}", f"mixa{gb}_{c}"], [f"mixa{gb}_{c}"])
                        nb = 4 * g + c + 2
                        if nb < 4 * NG:
                            load_sga(nb // 4, nb % 4)
                    deferred.append((n + 2, back_to_feature_major))

            def proj_bank(t, half):
                if state.get("tail"):
                    i = (2 * t + half) % 4
                    return scp[i // 2][:, i % 2, :], [f"fl{i}", f"scp{i // 2}"]
                return opb1[:], ["opb1"]

            def tile1_mm(t, half, cc):
                g, tt = divmod(t, 4)
                gb = g % 2
                rtoks = [f"mixp{gb}"] if cc < 4 else [f"mixa{gb}_{cc - 4}"]
                bank, btoks = proj_bank(t, half)
                _mm(P, bank, mixT[gb][:, cc, tt * 128:(tt + 1) * 128],
                    wout_bf[:, cc, half * 512:(half + 1) * 512], cc == 0, cc == 7,
                    rtoks + ["wout"], btoks)

            def tile1_half(t, half):
                z = zt[t % NXB]
                ztk = f"xtok{t % NXB}"
                hs = slice(half * 512, (half + 1) * 512)
                bank, btoks = proj_bank(t, half)
                _stt(P, z[:, hs], xtok[t % NXB][:, hs], ALPHA, bank, ALU.mult, ALU.add, btoks[:1] + [ztk], [ztk])

            def tile1_post(t):
                z = zt[t % NXB]
                ztk = f"xtok{t % NXB}"
                m = mv[t % NXB]
                mt = f"mv{t % NXB}"
                P.op("dve", lambda e: e.bn_stats(out=bst[:, 0:6], in_=z[:, 0:512]), r=[ztk, "bst"], w=["bst"])
                P.op("dve", lambda e: e.bn_stats(out=bst[:, 6:12], in_=z[:, 512:1024]), r=[ztk, "bst"], w=["bst"])
                P.op("dve", lambda e: e.bn_aggr(out=m[:, 0:2], in_=bst[:]), r=["bst", mt], w=[mt])
                _ts(P, "dve", m[:, 2:3], m[:, 1:2], LN_EPS, ALU.add, [mt], [mt])

            def tile2(t, flush=False):
                z = zt[t % NXB]
                ztk = f"xtok{t % NXB}"
                m = mv[t % NXB]
                mt = f"mv{t % NXB}"
                _act(P, m[:, 2:3], m[:, 2:3], AF.Ln, [mt], [mt])
                _act(P, m[:, 3:4], m[:, 2:3], AF.Exp, [mt], [mt], scale=-0.5)
                _ts(P, "dve", z[:], z[:], m[:, 0:1], ALU.subtract, [ztk, mt], [ztk], s2=m[:, 3:4], op1=ALU.mult)
                _tt(P, "dve" if flush else "pool", z[:], z[:], gain_b[:], ALU.mult, [ztk, "gain"], [ztk])
                _tt(P, "pool", z[:], z[:], bias_b[:], ALU.add, [ztk, "lbias"], [ztk])
                _dma(P, "sp", dap(out_d, t * 128 * D, [[D, 128], [1, D]]), z[:], "out", [ztk], [f"out{t}"])
                if t + NXB < NT:
                    load_x(t + NXB)

            its = []
            n = 0
            for g in range(NG):
                for c in range(4):
                    for kt in range(4 * g + 4):
                        its.append((g, c, kt, n))
                        n += 1
            load_qs(0)
            load_sga(0, 0)
            load_sga(0, 1)
            load_mixp(0)
            for t0 in range(NXB):
                load_x(t0)
            pending = []
            deferred = []
            posted = []
            t2_done = set()
            state = {"cool": 0, "idx": 0}
            post_iter = {}

            def do_tile2(t, flush=False):
                if t not in t2_done:
                    tile2(t, flush)
                    t2_done.add(t)

            def push_tile(t):
                for half in range(2):
                    for cc in range(8):
                        pending.append(("mm", t, half, cc))
                    pending.append(("half", t, half))
                pending.append(("fin", t))

            def pop_one(flush=False):
                item = pending.pop(0)
                if item[0] == "mm":
                    tile1_mm(item[1], item[2], item[3])
                    return False
                t = item[1]
                if item[0] == "half":
                    if item[2] == 0 and t >= NXB:
                        do_tile2(t - NXB, flush)
                    tile1_half(t, item[2])
                    state["cool"] = 1
                    return True
                tile1_post(t)
                posted.append(t)
                post_iter[t] = state["idx"]
                if t % 4 == 3 and t // 4 + 2 < NG:
                    load_mixp(t // 4 + 2)
                state["cool"] = 0
                return True

            def flush_groups_upto(gmax):
                while pending and pending[0][1] // 4 <= gmax:
                    pop_one()

            QK(its[0])
            QK(its[1])
            for idx, it in enumerate(its):
                g, c, kt, n = it
                state["idx"] = idx
                if c == 0 and kt == 0 and g + 1 < NG:
                    load_qs(g + 1)
                if idx + 2 < len(its):
                    QK(its[idx + 2])
                last = kt == 4 * g + 3
                if last:
                    flush_groups_upto(g - 2)
                burst = state.pop("burst", 0)
                while burst > 0 and pending:
                    burst -= 1
                    if pop_one():
                        break
                PV(it)
                while deferred and deferred[0][0] <= n:
                    deferred.pop(0)[1]()
                if state["cool"] > 0:
                    state["cool"] -= 1
                elif pending:
                    k = 2 if len(pending) > 22 else 1
                    for _ in range(k):
                        if not pending or pop_one():
                            break
                if last:
                    b = 4 * g + c
                    if b - 4 >= 0:
                        push_tile(b - 4)
                    state["burst"] = 8
                    for t in list(posted):
                        if idx - post_iter[t] >= 3:
                            do_tile2(t)
                    if b == 3 and NG > 1:
                        load_mixp(1)
            while deferred:
                deferred.pop(0)[1]()
            while pending:
                pop_one(flush=True)
            for t in list(posted):
                do_tile2(t, flush=True)
            state["tail"] = True
            for t in range(NT - 4, NT):
                push_tile(t)
                while pending:
                    pop_one(flush=True)
            for t in range(NT):
                do_tile2(t, flush=True)
            _run_block(nc, P, _sem_names(P), g_es, "b")
    return nc


_NC_CACHE = {}


def kernel(x, positions, w_in, pool_w, pool_scale, w_out, ln_gain, ln_bias):
    x = np.asarray(x, dtype=np.float32)
    positions = np.asarray(positions, dtype=np.int32)
    w_in = np.ascontiguousarray(np.asarray(w_in, dtype=np.float32)[0])
    pool_w = np.ascontiguousarray(np.asarray(pool_w, dtype=np.float32)[0])
    pool_scale = np.ascontiguousarray(np.asarray(pool_scale, dtype=np.float32)[0].reshape(4, 128).T)
    w_out = np.ascontiguousarray(np.asarray(w_out, dtype=np.float32)[0])
    ln_gain = np.ascontiguousarray(np.asarray(ln_gain, dtype=np.float32)[0].reshape(1, D))
    ln_bias = np.ascontiguousarray(np.asarray(ln_bias, dtype=np.float32)[0].reshape(1, D))
    if "nc" not in _NC_CACHE:
        _NC_CACHE["nc"] = build_nc()
    nc = _NC_CACHE["nc"]
    in_maps = []
    for b in range(8):
        xb = np.ascontiguousarray(x[b])
        in_maps.append({
            "xT": np.ascontiguousarray(xb.T),
            "x": xb,
            "pos": np.ascontiguousarray(positions[b].reshape(NT, 128).T),
            "w_in": w_in, "pool_w": pool_w, "pool_scale": pool_scale, "w_out": w_out,
            "ln_gain": ln_gain, "ln_bias": ln_bias,
        })
    res = run_bass_kernel_spmd(nc, in_maps, core_ids=list(range(8)))
    if DEBUG:
        kernel.last = res
    return np.stack([np.asarray(r["out"], dtype=np.float32) for r in res.results], axis=0)
```

```python
import math
from contextlib import ExitStack

import numpy as np
import concourse.bass as bass
import concourse.mybir as mybir
from concourse.bass_utils import run_bass_kernel_spmd

F32 = mybir.dt.float32
BF16 = mybir.dt.bfloat16
I32 = mybir.dt.int32
AF = mybir.ActivationFunctionType
ALU = mybir.AluOpType
AX = mybir.AxisListType

S = 4096
D = 1024
NT = S // 128
NG = S // 512
NEG = -30000.0
ALPHA = 2.0 ** 0.25
LN_EPS = 1e-5
DEBUG = False


class Prog:
    def __init__(self):
        self.ops = []

    def op(self, eng, fn, r=(), w=(), dma=None):
        self.ops.append(dict(eng=eng, fn=fn, r=tuple(r), w=tuple(w), dma=dma,
                             deps=set(), sig=dma is not None))
        return len(self.ops) - 1

    def analyze(self):
        last_w = {}
        readers = {}
        for i, o in enumerate(self.ops):
            deps = set()
            for t in o["r"]:
                if t in last_w:
                    deps.add(last_w[t])
            for t in o["w"]:
                if t in last_w:
                    deps.add(last_w[t])
                for j in readers.get(t, ()):
                    deps.add(j)
            deps.discard(i)
            for t in o["w"]:
                last_w[t] = i
                readers[t] = []
            for t in o["r"]:
                if t not in o["w"]:
                    readers.setdefault(t, []).append(i)
            keep = set()
            for j in deps:
                p = self.ops[j]
                if p["dma"] is None and p["eng"] == "pe" and o["eng"] == "pe" and o["dma"] is None:
                    continue
                keep.add(j)
            o["deps"] = keep
            for j in keep:
                self.ops[j]["sig"] = True

    def emit(self, sems):
        self.analyze()
        cnt = {}
        for o in self.ops:
            key = o["dma"] if o["dma"] is not None else o["eng"]
            o["key"] = key
            if o["sig"]:
                cnt[key] = cnt.get(key, 0) + 1
                o["count"] = cnt[key] * (16 if o["dma"] is not None else 1)
        streams = {}
        for i, o in enumerate(self.ops):
            streams.setdefault(o["eng"], []).append(i)
        self.total = dict(cnt)

        def run_stream(engname, eng):
            waited = {}
            for i in streams.get(engname, []):
                o = self.ops[i]
                need = {}
                for j in o["deps"]:
                    p = self.ops[j]
                    k = p["key"]
                    need[k] = max(need.get(k, 0), p["count"])
                for k, v in need.items():
                    if waited.get(k, 0) < v:
                        eng.wait_ge(sems[k], v)
                        waited[k] = v
                ins = o["fn"](eng)
                if o["sig"]:
                    ins.then_inc(sems[o["key"]], 16 if o["dma"] is not None else 1)

        return run_stream


def _mm(P, out, lhsT, rhs, start, stop, r, w, tp=None):
    def fn(e):
        if tp is None:
            return e.matmul(out, lhsT=lhsT, rhs=rhs, start=start, stop=stop)
        return e.matmul(out, lhsT=lhsT, rhs=rhs, start=start, stop=stop, tile_position=tp)
    P.op("pe", fn, r=r, w=w)


def _tr(P, out, in_, ident, r, w):
    P.op("pe", lambda e: e.transpose(out=out, in_=in_, identity=ident), r=r, w=w)


def _act(P, out, in_, func, r, w, scale=None):
    def fn(e):
        if scale is None:
            return e.activation(out=out, in_=in_, func=func)
        return e.activation(out=out, in_=in_, func=func, scale=scale)
    P.op("act", fn, r=r, w=w)


def _copy(P, eng, out, in_, r, w):
    if eng == "act":
        _act(P, out, in_, AF.Copy, r, w)
    else:
        P.op(eng, lambda e: e.tensor_copy(out=out, in_=in_), r=r, w=w)


def _tt(P, eng, out, in0, in1, op, r, w):
    P.op(eng, lambda e: e.tensor_tensor(out=out, in0=in0, in1=in1, op=op), r=r, w=w)


def _ts(P, eng, out, in0, s1, op0, r, w, s2=None, op1=None):
    def fn(e):
        if op1 is None:
            return e.tensor_scalar(out=out, in0=in0, scalar1=s1, scalar2=None, op0=op0)
        return e.tensor_scalar(out=out, in0=in0, scalar1=s1, scalar2=s2, op0=op0, op1=op1)
    P.op(eng, fn, r=r, w=w)


def _stt(P, out, in0, scalar, in1, op0, op1, r, w):
    P.op("dve", lambda e: e.scalar_tensor_tensor(out=out, in0=in0, scalar=scalar, in1=in1, op0=op0, op1=op1),
         r=r, w=w)


def _memset(P, eng, ap, val, r, w):
    P.op(eng, lambda e: e.memset(ap, val), r=r, w=w)


def _dma(P, eng, out, in_, stream, r, w):
    key = ("L:" + w[0]) if not r else ("S:" + r[0])
    P.op(eng, lambda e: e.dma_start(out=out, in_=in_), r=r, w=w, dma=key)


def _run_block(nc, P, sem_names, es, tag):
    sems = {n: es.enter_context(nc.semaphore(tag + "_" + n.replace(":", "_"))) for n in sem_names}
    run = P.emit(sems)
    with nc.Block() as block:
        @block.sync
        def _(e):
            run("sp", e)
            for k, v in P.total.items():
                if k not in ("pe", "act", "dve", "pool"):
                    e.wait_ge(sems[k], 16 * v)

        @block.tensor
        def _(e):
            run("pe", e)

        @block.scalar
        def _(e):
            run("act", e)

        @block.vector
        def _(e):
            run("dve", e)

        @block.gpsimd
        def _(e):
            run("pool", e)


def _sem_names(P):
    names = {"pe", "act", "dve", "pool"}
    for o in P.ops:
        if o["dma"] is not None:
            names.add(o["dma"])
    return sorted(names)


def build_nc():
    nc = bass.Bass("TRN2", target_bir_lowering=False)
    xT_d = nc.dram_tensor("xT", [D, S], F32, kind="ExternalInput")
    x_d = nc.dram_tensor("x", [S, D], F32, kind="ExternalInput")
    pos_d = nc.dram_tensor("pos", [128, NT], I32, kind="ExternalInput")
    win_d = nc.dram_tensor("w_in", [D, 3072], F32, kind="ExternalInput")
    poolw_d = nc.dram_tensor("pool_w", [4, 128, 128], F32, kind="ExternalInput")
    pscale_d = nc.dram_tensor("pool_scale", [128, 4], F32, kind="ExternalInput")
    wout_d = nc.dram_tensor("w_out", [D, D], F32, kind="ExternalInput")
    gain_d = nc.dram_tensor("ln_gain", [1, D], F32, kind="ExternalInput")
    lbias_d = nc.dram_tensor("ln_bias", [1, D], F32, kind="ExternalInput")
    out_d = nc.dram_tensor("out", [S, D], F32, kind="ExternalOutput")
    skind = "ExternalOutput" if DEBUG else "Internal"
    qsp_d = nc.dram_tensor("q_sp", [8, 80, S], BF16, kind=skind)
    gsp_d = nc.dram_tensor("g_sp", [512, S], F32, kind=skind)
    mpsp_d = nc.dram_tensor("mp_sp", [512, S], BF16, kind=skind)
    if DEBUG:
        kdbg_d = nc.dram_tensor("k_dbg", [80, 8 * S], BF16, kind="ExternalOutput")
        vdbg_d = nc.dram_tensor("v_dbg", [128, NT * 8 * 96], BF16, kind="ExternalOutput")

    def dap(t, off, pat):
        return bass.AP(t, off, pat)

    with ExitStack() as g_es:
        def gsb(name, shape, dt):
            return g_es.enter_context(nc.sbuf_tensor(name, shape, dt))
        kaug = gsb("kaug", [128, 8, S], BF16)
        vaug = gsb("vaug", [128, NT, 8, 96], BF16)
        wbuf = gsb("wbuf", [128, 8, 1536], BF16)
        xg = gsb("xg", [128, 4 * D], F32)
        xTbf_g = [xg[:, j * 2048:(j + 1) * 2048].bitcast(BF16).rearrange("p (k n) -> p k n", k=8) for j in range(2)]
        ident = gsb("ident", [128, 128], BF16)
        tribias = gsb("tribias", [128, 128], BF16)
        ones_bf = gsb("ones_bf", [128, 128], BF16)
        zeros_bf = gsb("zeros_bf", [128, 128], BF16)
        cosT = gsb("cosT", [128, NT, 8], F32)
        sinT = gsb("sinT", [128, NT, 8], F32)

        with ExitStack() as es:
            def sb(name, shape, dt):
                return es.enter_context(nc.sbuf_tensor(name, shape, dt))

            def ps(name, shape, dt=F32):
                return es.enter_context(nc.psum_tensor(name, shape, dt))
            P = Prog()
            w_fm = wbuf
            xTbf = xTbf_g
            poolw_bf = sb("poolw_bf", [128, 4, 128], BF16)
            pscale = sb("pscale", [128, 4], F32)
            ubuf = sb("ubuf", [128, 4, 528], F32)
            Sa = sb("Sa", [128, 528], F32)
            Sb = sb("Sb", [128, 528], F32)
            d_bf = sb("d_bf", [128, 4, 512], BF16)
            sg = [sb(f"sg{i}", [128, 512], F32) for i in range(2)]
            mp_stage = [sb(f"mp_stage{i}", [128, 4, 512], BF16) for i in range(2)]
            sga_stage = [sb(f"sga_stage{i}", [128, 512], F32) for i in range(2)]
            rcw = sb("rcw", [128, 16], F32)
            fix = sb("fix", [128, 16], F32)
            posi = sb("posi", [128, NT], I32)
            posf = sb("posf", [128, NT], F32)
            ang = sb("ang", [128, NT * 8], F32)
            angc = sb("angc", [128, NT * 8], F32)
            yy = sb("yy", [128, NT * 8], F32)
            ki = sb("ki", [128, NT * 8], I32)
            kf = sb("kf", [128, NT * 8], F32)
            mk = sb("mk", [128, NT * 8], F32)
            pj = [ps(f"pjA{i}", [128, 512]) for i in range(4)]
            yp = [ps(f"ypA{i}", [128, 512]) for i in range(2)]

            def xload_a1(g):
                _dma(P, "pool", xTbf[g % 2], dap(xT_d, g * 512, [[S, 128], [128 * S, 8], [1, 512]]), "xld",
                     [], [f"xTA{g % 2}"])
            xload_a1(0)
            for k0, k1 in ((0, 4), (4, 8)):
                nk = k1 - k0
                _dma(P, "pool", w_fm[:, k0:k1, 0:1024], dap(win_d, k0 * 128 * 3072, [[3072, 128], [128 * 3072, nk], [1, 1024]]),
                     "wld", [], [f"w_fm{kc}a" for kc in range(k0, k1)])
            for k0, k1 in ((0, 4), (4, 8)):
                nk = k1 - k0
                _dma(P, "pool", w_fm[:, k0:k1, 1024:1536], dap(win_d, k0 * 128 * 3072 + 2560, [[3072, 128], [128 * 3072, nk], [1, 512]]),
                     "wld", [], [f"w_fm{kc}b" for kc in range(k0, k1)])
            _dma(P, "pool", poolw_bf[:], dap(poolw_d, 0, [[128, 128], [128 * 128, 4], [1, 128]]), "wld", [], ["poolw"])
            _dma(P, "sp", pscale[:], pscale_d.ap(), "misc", [], ["pscale"])
            _dma(P, "sp", posi[:], pos_d.ap(), "misc", [], ["posi"])

            _memset(P, "pool", ones_bf[:], 1.0, [], ["ones"])
            _memset(P, "pool", zeros_bf[:], 0.0, [], ["zeros"])
            P.op("pool", lambda e: e.affine_select(out=ident[:], in_=ones_bf[:], pattern=[[-1, 128]],
                                                   compare_op=ALU.is_equal, fill=0.0, base=0, channel_multiplier=1),
                 r=["ones"], w=["ident"])
            P.op("pool", lambda e: e.affine_select(out=tribias[:], in_=zeros_bf[:], pattern=[[1, 128]],
                                                   compare_op=ALU.is_ge, fill=NEG, base=0, channel_multiplier=-1),
                 r=["zeros"], w=["tribias"])
            for t in range(15):
                _memset(P, "pool", rcw[:, t:t + 1], 1.0 / (t + 1), ["rcw"], ["rcw"])
            _memset(P, "pool", rcw[:, 15:16], 1.0 / 16, ["rcw"], ["rcw"])
            freqs = (np.float32(500000.0) ** (-(np.arange(8, dtype=np.float32) * np.float32(2.0)) / np.float32(16.0))).astype(np.float32)
            _copy(P, "dve", posf[:], posi[:], ["posi"], ["posf"])
            ang3 = ang[:].rearrange("p (t j) -> p t j", j=8)
            for j in range(8):
                _ts(P, "dve", ang3[:, :, j], posf[:], float(freqs[j]), ALU.mult, ["posf", "ang"], ["ang"])
            _ts(P, "dve", angc[:], ang[:], math.pi / 2, ALU.add, ["ang"], ["angc"])
            C1 = 6.28125
            C2 = 2 * math.pi - 6.28125

            def sin_of(src, srctok, dst):
                _ts(P, "dve", yy[:], src[:], 1.0 / (2 * math.pi), ALU.mult, [srctok, "yy"], ["yy"])
                _copy(P, "dve", ki[:], yy[:], ["yy", "ki"], ["ki"])
                _copy(P, "dve", kf[:], ki[:], ["ki", "kf"], ["kf"])
                _stt(P, src[:], kf[:], -C1, src[:], ALU.mult, ALU.add, ["kf", srctok], [srctok])
                _stt(P, src[:], kf[:], -C2, src[:], ALU.mult, ALU.add, ["kf", srctok], [srctok])
                _ts(P, "dve", mk[:], src[:], math.pi, ALU.is_gt, [srctok, "mk"], ["mk"])
                _stt(P, src[:], mk[:], -2 * math.pi, src[:], ALU.mult, ALU.add, ["mk", srctok], [srctok])
                _ts(P, "dve", src[:], src[:], -math.pi, ALU.max, [srctok], [srctok], s2=math.pi, op1=ALU.min)
                _act(P, dst[:].rearrange("p t j -> p (t j)"), src[:], AF.Sin, [srctok], ["rot_tab"])
            sin_of(ang, "ang", sinT)
            sin_of(angc, "angc", cosT)

            chunk_seq = [("u", 0), ("gp", 0), ("u", 1), ("gp", 1), ("u", 2), ("gp", 2), ("u", 3), ("gp", 3),
                         ("ga", 0), ("ga", 1), ("ga", 2), ("ga", 3)]
            nbank = 0

            for g in range(NG):
                xb = g % 2
                if g + 1 < NG:
                    xload_a1(g + 1)
                else:
                    _dma(P, "pool", xTbf[0], dap(xT_d, 0, [[S, 128], [128 * S, 8], [1, 512]]), "xld", [], ["xTA0"])
                pend_pool_mm = []

                def pool_mm(gi, g=g, xb=xb):
                    _mm(P, yp[gi % 2][:], poolw_bf[:, gi, :], d_bf[:, gi, :], True, True,
                        ["poolw", f"d{gi}"], [f"yp{gi % 2}"])
                    _stt(P, mp_stage[xb][:, gi, :], yp[gi % 2][:], pscale[:, gi:gi + 1], sg[gi % 2][:],
                         ALU.mult, ALU.mult, [f"yp{gi % 2}", "pscale", f"sg{gi % 2}"], [f"mp{xb}_{gi}"])

                for kind, idx in chunk_seq:
                    bank = pj[nbank % 4]
                    btok = f"pj{nbank % 4}"
                    nbank += 1
                    col0 = {"u": idx * 128, "gp": 512 + idx * 128, "ga": 1024 + idx * 128}[kind]
                    for kc in range(8):
                        _mm(P, bank[:], w_fm[:, kc, col0:col0 + 128], xTbf[xb][:, kc, :], kc == 0, kc == 7,
                            [f"w_fm{kc}b" if kind == "ga" else f"w_fm{kc}a", f"xTA{xb}"], [btok])
                    if kind == "u":
                        gi = idx
                        w = 2 << gi
                        ut = f"ubuf{gi}"
                        if len(pend_pool_mm) >= 2:
                            pool_mm(pend_pool_mm.pop(0))
                        if g == 0:
                            _memset(P, "pool", ubuf[:, gi, 0:16], 0.0, [ut], [ut])
                        else:
                            _copy(P, "pool", ubuf[:, gi, 0:16], ubuf[:, gi, 512:528], [ut], [ut])
                        _copy(P, "act", ubuf[:, gi, 16:528], bank[:], [btok, ut], [ut])
                        _tt(P, "pool", Sa[:, 1:528], ubuf[:, gi, 1:528], ubuf[:, gi, 0:527], ALU.add, [ut, "Sa"], ["Sa"])
                        cur, curt = Sa, "Sa"
                        if w >= 4:
                            _tt(P, "pool", Sb[:, 3:528], Sa[:, 3:528], Sa[:, 1:526], ALU.add, ["Sa", "Sb"], ["Sb"])
                            cur, curt = Sb, "Sb"
                        if w >= 8:
                            _tt(P, "pool", Sa[:, 7:528], Sb[:, 7:528], Sb[:, 3:524], ALU.add, ["Sb", "Sa"], ["Sa"])
                            cur, curt = Sa, "Sa"
                        if w >= 16:
                            _tt(P, "pool", Sb[:, 15:528], Sa[:, 15:528], Sa[:, 7:520], ALU.add, ["Sa", "Sb"], ["Sb"])
                            cur, curt = Sb, "Sb"
                        _stt(P, d_bf[:, gi, :], cur[:, 16:528], 1.0 / w, ubuf[:, gi, 16:528], ALU.mult, ALU.subtract,
                             [curt, ut, f"d{gi}"], [f"d{gi}"])
                        if g == 0:
                            _tt(P, "dve", fix[:, 0:w - 1], cur[:, 16:16 + w - 1], rcw[:, 0:w - 1], ALU.mult,
                                [curt, "rcw", "fix"], ["fix"])
                            _tt(P, "dve", d_bf[:, gi, 0:w - 1], fix[:, 0:w - 1], ubuf[:, gi, 16:16 + w - 1], ALU.subtract,
                                ["fix", ut, f"d{gi}"], [f"d{gi}"])
                        pend_pool_mm.append(gi)
                    elif kind == "gp":
                        gi = idx
                        _act(P, sg[gi % 2][:], bank[:], AF.Silu, [btok, f"sg{gi % 2}"], [f"sg{gi % 2}"])
                        if g == NG - 1 and gi == 3:
                            _dma(P, "pool", wbuf[:, :, 0:1024], dap(win_d, 1024, [[3072, 128], [128 * 3072, 8], [1, 1024]]),
                                 "wld", [], [f"w_fm{kc}a" for kc in range(8)])
                    else:
                        j = idx
                        st = sga_stage[j % 2]
                        _act(P, st[:], bank[:], AF.Silu, [btok, f"sga{j % 2}"], [f"sga{j % 2}"])
                        _dma(P, "sp", dap(gsp_d, (j * 128) * S + g * 512, [[S, 128], [1, 512]]), st[:], "spw",
                             [f"sga{j % 2}"], [f"gsp{g}"])
                        if pend_pool_mm and j in (0, 2):
                            pool_mm(pend_pool_mm.pop(0))
                while pend_pool_mm:
                    pool_mm(pend_pool_mm.pop(0))
                _dma(P, "sp", dap(mpsp_d, g * 512, [[S, 128], [128 * S, 4], [1, 512]]), mp_stage[xb][:], "spw",
                     [f"mp{xb}_{i}" for i in range(4)], [f"mpsp{g}"])
            _dma(P, "pool", wbuf[:, :, 1024:1536], dap(win_d, 2048, [[3072, 128], [128 * 3072, 8], [1, 512]]),
                 "wld", [], [f"w_fm{kc}b" for kc in range(8)])
            _run_block(nc, P, _sem_names(P), g_es, "a1")

        with ExitStack() as es:
            def sb(name, shape, dt):
                return es.enter_context(nc.sbuf_tensor(name, shape, dt))

            def ps(name, shape, dt=F32):
                return es.enter_context(nc.psum_tensor(name, shape, dt))
            P = Prog()
            w_tm = wbuf
            xTbf = xTbf_g
            NQK = 4
            qk_tok = [sb(f"qk_tok{i}", [128, 16, 80], BF16) for i in range(NQK)]
            rt = [sb(f"rt{i}", [128, 16, 8], F32) for i in range(4)]
            qTg = [sb(f"qTg{i}", [128, 8, 128], BF16) for i in range(2)]
            gm = sb("gm", [128, 8, 16], F32)
            top8 = sb("top8", [128, 8, 8], F32)
            sel = sb("sel", [128, 8, 16], F32)
            ksum = sb("ksum", [128, 8], F32)
            kmT = sb("kmT", [128, 8, 16], BF16)
            qstage = [sb(f"qstage{i}", [128, 8, 512], BF16) for i in range(2)]
            qb = ps("qbank", [128, 512])
            kb = ps("kbank", [128, 512])
            vb = ps("vbank", [128, 512])
            NTR = 4
            tr = [ps(f"trb{i}", [128, 1024], BF16) for i in range(NTR)]
            gt = ps("gtbank", [128, 512])

            _memset(P, "dve", gm[:], -1.0e30, [], ["gm"])
            for t8 in range(0, NT, 8):
                _memset(P, "pool", vaug[:, t8:t8 + 8, :, 64:96], 1.0, [], ["vones"])
            ntr_box = [0]
            trk_of = {}
            trq_of = {}

            def next_tr():
                i = ntr_box[0] % NTR
                ntr_box[0] += 1
                return tr[i], f"tr{i}"

            def xload_a2(g):
                _dma(P, "pool", xTbf[g % 2], dap(xT_d, g * 512, [[S, 128], [128 * S, 8], [1, 512]]), "xld",
                     [], [f"xTB{g % 2}"])

            def S12(t):
                g, tt = divmod(t, 4)
                xb = g % 2
                cur = t // 2
                qk = qk_tok[t % NQK]
                qkt = f"qk{t % NQK}"
                for ci, (bank, btok) in enumerate(((qb, "qb"), (kb, "kb"), (vb, "vb"))):
                    for kc in range(8):
                        _mm(P, bank[:], xTbf[xb][:, kc, tt * 128:(tt + 1) * 128], w_tm[:, kc, ci * 512:(ci + 1) * 512],
                            kc == 0, kc == 7, [f"xTB{xb}", f"w_tm{kc}"], [btok])
                cosb = cosT[:, t:t + 1, :].to_broadcast([128, 8, 8])
                sinb = sinT[:, t:t + 1, :].to_broadcast([128, 8, 8])
                for bank, btok, h0 in ((qb, "qb", 0), (kb, "kb", 8)):
                    X = bank[:].rearrange("p (h d) -> p h d", d=64)
                    x1 = X[:, :, 0:8]
                    x2 = X[:, :, 8:16]
                    hs = slice(h0, h0 + 8)
                    _tt(P, "dve", rt[0][:, hs, :], x1, cosb, ALU.mult, [btok, "rot_tab", "rt0"], ["rt0"])
                    _tt(P, "dve", rt[1][:, hs, :], x2, sinb, ALU.mult, [btok, "rot_tab", "rt1"], ["rt1"])
                    _tt(P, "dve", rt[2][:, hs, :], x2, cosb, ALU.mult, [btok, "rot_tab", "rt2"], ["rt2"])
                    _tt(P, "dve", rt[3][:, hs, :], x1, sinb, ALU.mult, [btok, "rot_tab", "rt3"], ["rt3"])
                    _tt(P, "dve", qk[:, hs, 0:8], rt[0][:, hs, :], rt[1][:, hs, :], ALU.subtract, ["rt0", "rt1", qkt], [qkt])
                    _tt(P, "dve", qk[:, hs, 8:16], rt[2][:, hs, :], rt[3][:, hs, :], ALU.add, ["rt2", "rt3", qkt], [qkt])
                    _copy(P, "act", qk[:, hs, 16:64], X[:, :, 16:64], [btok, qkt], [qkt])
                _copy(P, "act", vaug[:, t, :, 0:64], vb[:].rearrange("p (h d) -> p h d", d=64), ["vb"], [f"v{t}"])
                _memset(P, "pool", qk[:, 8:16, 64:80], 0.0, [qkt], [qkt])
                _memset(P, "pool", qk[:, 8:16, 64 + cur:65 + cur], 1.0, [qkt], [qkt])
                if cur == 0:
                    _memset(P, "pool", qk[:, 0:8, 64:80], 0.0, [qkt], [qkt])

            def S34(t):
                cur = t // 2
                qk = qk_tok[t % NQK]
                qkt = f"qk{t % NQK}"
                trk, trkt = next_tr()
                for h in range(8):
                    _tr(P, trk[0:80, h * 128:(h + 1) * 128], qk[:, 8 + h, 0:80], ident[:], [qkt, "ident"], [trkt])
                if cur > 0:
                    trq, trqt = next_tr()
                    for h in range(8):
                        _tr(P, trq[0:64, h * 128:(h + 1) * 128], qk[:, h, 0:64], ident[:], [qkt, "ident"], [trqt])
                _copy(P, "act", kaug[0:80, :, t * 128:(t + 1) * 128], trk[0:80, :].rearrange("p (h n) -> p h n", h=8),
                      [trkt], [f"kaug{t}"])
                if cur > 0:
                    _copy(P, "dve", qTg[t % 2][0:64, :, :], trq[0:64, :].rearrange("p (h n) -> p h n", h=8),
                          [trqt, f"qTg{t % 2}"], [f"qTg{t % 2}"])
                for h in range(8):
                    _mm(P, gt[0:64, 128 + h:129 + h], qk[:, 8 + h, 0:64], ones_bf[:, 0:1], True, True, [qkt], ["gt"])
                if t % 2 == 0:
                    _copy(P, "act", ksum[0:64, :], gt[0:64, 128:136], ["gt", "ksum"], ["ksum"])
                else:
                    _tt(P, "dve", kmT[0:64, :, cur], ksum[0:64, :], gt[0:64, 128:136], ALU.add, ["ksum", "gt"], [f"km{cur}"])

            def S56(t):
                cur = t // 2
                if cur == 0:
                    return
                qk = qk_tok[t % NQK]
                qkt = f"qk{t % NQK}"
                for h in range(8):
                    _mm(P, gt[:, h * 16:h * 16 + cur], qTg[t % 2][0:64, h, :], kmT[0:64, h, 0:cur], True, True,
                        [f"qTg{t % 2}"] + [f"km{n}" for n in range(cur)], ["gt"])
                gt3 = gt[:, 0:128].rearrange("p (h n) -> p h n", n=16)
                _copy(P, "dve", gm[:, :, 0:cur], gt3[:, :, 0:cur], ["gt", "gm"], ["gm"])
                for h in range(8):
                    P.op("dve", (lambda h: lambda e: e.max(out=top8[:, h, :], in_=gm[:, h, :]))(h),
                         r=["gm", "top8"], w=["top8"])
                _tt(P, "dve", sel[:], gm[:], top8[:, :, 2:3].to_broadcast([128, 8, 16]), ALU.is_ge,
                    ["gm", "top8", "sel"], ["sel"])
                _ts(P, "dve", qk[:, 0:8, 64:80], sel[:], -1.0, ALU.add, ["sel", qkt], [qkt], s2=-NEG, op1=ALU.mult)
                _memset(P, "pool", qk[:, 0:8, 64 + cur:65 + cur], 0.0, [qkt], [qkt])

            def S78(t):
                g, tt = divmod(t, 4)
                xb = g % 2
                qk = qk_tok[t % NQK]
                qkt = f"qk{t % NQK}"
                trf, trft = next_tr()
                for h in range(8):
                    _tr(P, trf[0:80, h * 128:(h + 1) * 128], qk[:, h, 0:80], ident[:], [qkt, "ident"], [trft])
                _copy(P, "act", qstage[xb][0:80, :, tt * 128:(tt + 1) * 128],
                      trf[0:80, :].rearrange("p (h n) -> p h n", h=8), [trft, f"qst{xb}"], [f"qst{xb}"])
                if tt == 3:
                    _dma(P, "sp", dap(qsp_d, g * 512, [[S, 80], [80 * S, 8], [1, 512]]), qstage[xb][0:80, :, :], "spw",
                         [f"qst{xb}"], [f"qsp{g}"])

            for i in range(NT + 2):
                if i % 4 == 0 and i // 4 + 1 < NG and i < NT:
                    xload_a2(i // 4 + 1)
                if 0 <= i - 2 < NT:
                    S56(i - 2)
                if 0 <= i - 1 < NT:
                    S34(i - 1)
                if i < NT:
                    S12(i)
                if 0 <= i - 2 < NT:
                    S78(i - 2)
            if DEBUG:
                _dma(P, "sp", kdbg_d.ap(), kaug[0:80, :, :].rearrange("p h n -> p (h n)"), "spw",
                     [f"kaug{t}" for t in range(NT)], ["kdbg"])
                _dma(P, "sp", vdbg_d.ap(), vaug[:].rearrange("p t h n -> p (t h n)"), "spw",
                     [f"v{t}" for t in range(NT)], ["vdbg"])
            _run_block(nc, P, _sem_names(P), g_es, "a2")

        with ExitStack() as es:
            def sb(name, shape, dt):
                return es.enter_context(nc.sbuf_tensor(name, shape, dt))

            def ps(name, shape, dt=F32):
                return es.enter_context(nc.psum_tensor(name, shape, dt))
            P = Prog()
            wout_bf = wbuf[:, :, 0:D]
            gain_b = sb("gain_b", [128, D], F32)
            bias_b = sb("bias_b", [128, D], F32)
            qaug = [sb(f"qaug{i}", [128, 8, 512], BF16) for i in range(2)]
            mixT = [sb(f"mixT{i}", [128, 8, 512], BF16) for i in range(2)]
            sga = [sb(f"sga{i}", [128, 512], F32) for i in range(2)]
            PT = [sb(f"PT{i}", [128, 2, 512], BF16) for i in range(3)]
            ontok = sb("ontok", [128, 4, 128], BF16)
            rdn = sb("rdn", [128, 8], F32)
            NXB = 4
            xtok = [xg[:, i * D:(i + 1) * D] for i in range(NXB)]
            zt = xtok
            bst = sb("bst", [128, 12], F32)
            mv = [sb(f"mv{i}", [128, 4], F32) for i in range(NXB)]
            scp = [ps(f"scp{i}", [128, 2, 512]) for i in range(2)]
            accT = [ps(f"accT{i}", [128, 512]) for i in range(2)]
            opb1 = ps("opb1", [128, 512])
            trp = ps("trp", [128, 1024], BF16)

            _dma(P, "pool", wout_bf, dap(wout_d, 0, [[D, 128], [128 * D, 8], [1, D]]), "wld", [], ["wout"])
            _dma(P, "sp", gain_b[:], dap(gain_d, 0, [[0, 128], [1, D]]), "misc", [], ["gain"])
            _dma(P, "sp", bias_b[:], dap(lbias_d, 0, [[0, 128], [1, D]]), "misc", [], ["lbias"])

            def load_qs(g):
                gb = g % 2
                _dma(P, "sp", qaug[gb][0:80, :, :], dap(qsp_d, g * 512, [[S, 80], [80 * S, 8], [1, 512]]), "qld",
                     [], [f"qaug{gb}"])

            def load_sga(g, c):
                k = (4 * g + c) % 2
                _dma(P, "sp", sga[k][:], dap(gsp_d, (c * 128) * S + g * 512, [[S, 128], [1, 512]]), "qld",
                     [], [f"sga{k}"])

            def load_mixp(g):
                gb = g % 2
                _dma(P, "sp", mixT[gb][:, 0:4, :], dap(mpsp_d, g * 512, [[S, 128], [128 * S, 4], [1, 512]]), "qld",
                     [], [f"mixp{gb}"])

            def load_x(t):
                _dma(P, "sp", xtok[t % NXB][:], dap(x_d, t * 128 * D, [[D, 128], [1, D]]), "xres", [], [f"xtok{t % NXB}"])

            def QK(it):
                g, c, kt, n = it
                gb = g % 2
                j = kt - 4 * g
                c0 = 0 if j < 0 else 128 * j
                sc = scp[n % 2]
                sct = f"scp{n % 2}"
                for hi, h in enumerate((2 * c, 2 * c + 1)):
                    _mm(P, sc[:, hi, c0:512], kaug[0:80, h, kt * 128:(kt + 1) * 128], qaug[gb][0:80, h, c0:512],
                        True, j < 0, [f"qaug{gb}"], [sct])
                    if j >= 0:
                        _mm(P, sc[:, hi, c0:c0 + 128], ident[:], tribias[:], False, True, [], [sct])
                pt = PT[n % 3]
                ptt = f"PT{n % 3}"
                _act(P, pt[:, :, c0:512], sc[:, :, c0:512], AF.Exp, [sct, ptt], [ptt], scale=0.125)

            def PV(it):
                g, c, kt, n = it
                gb = g % 2
                j = kt - 4 * g
                s0 = 0 if j < 0 else j
                A, B_ = 2 * c, 2 * c + 1
                pt = PT[n % 3]
                ptt = f"PT{n % 3}"
                first = kt == 0
                last = kt == 4 * g + 3
                for hi, h in enumerate((A, B_)):
                    for sq in range(s0, 4):
                        _mm(P, accT[hi][:, sq * 128:sq * 128 + 65], pt[:, hi, sq * 128:(sq + 1) * 128], vaug[:, kt, h, 0:65],
                            first and sq == 0, last and sq == 3, [ptt], [f"accT{hi}"])
                if last:
                    k = (4 * g + c) % 2
                    XA = accT[0][:].rearrange("p (s c) -> p s c", c=128)
                    XB = accT[1][:].rearrange("p (s c) -> p s c", c=128)
                    P.op("dve", lambda e: e.reciprocal(out=rdn[:, 0:4], in_=XA[:, :, 64]), r=["accT0", "rdn"], w=["rdn"])
                    P.op("dve", lambda e: e.reciprocal(out=rdn[:, 4:8], in_=XB[:, :, 64]), r=["accT1", "rdn"], w=["rdn"])
                    rA = rdn[:, 0:4].rearrange("p (s o) -> p s o", o=1).to_broadcast([128, 4, 64])
                    rB = rdn[:, 4:8].rearrange("p (s o) -> p s o", o=1).to_broadcast([128, 4, 64])
                    _tt(P, "dve", ontok[:, :, 0:64], XA[:, :, 0:64], rA, ALU.mult, ["accT0", "rdn", "ontok"], ["ontok"])
                    _tt(P, "dve", ontok[:, :, 64:128], XB[:, :, 0:64], rB, ALU.mult, ["accT1", "rdn", "ontok"], ["ontok"])

                    def back_to_feature_major(g=g, c=c, gb=gb, k=k):
                        for sq in range(4):
                            _tr(P, trp[:, sq * 128:(sq + 1) * 128], ontok[:, sq, :], ident[:], ["ontok", "ident"], ["trp"])
                        _tt(P, "dve", mixT[gb][:, 4 + c, :], trp[:, 0:512], sga[k][:], ALU.mult,
                            ["trp", f"sga{k}", f"mixa{gb}_{c}"], [f"mixa{gb}_{c}"])
                        nb = 4 * g + c + 2
                        if nb < 4 * NG:
                            load_sga(nb // 4, nb % 4)
                    deferred.append((n + 2, back_to_feature_major))

            def proj_bank(t, half):
                if state.get("tail"):
                    i = (2 * t + half) % 4
                    return scp[i // 2][:, i % 2, :], [f"fl{i}", f"scp{i // 2}"]
                return opb1[:], ["opb1"]

            def tile1_mm(t, half, cc):
                g, tt = divmod(t, 4)
                gb = g % 2
                rtoks = [f"mixp{gb}"] if cc < 4 else [f"mixa{gb}_{cc - 4}"]
                bank, btoks = proj_bank(t, half)
                _mm(P, bank, mixT[gb][:, cc, tt * 128:(tt + 1) * 128],
                    wout_bf[:, cc, half * 512:(half + 1) * 512], cc == 0, cc == 7,
                    rtoks + ["wout"], btoks)

            def tile1_half(t, half):
                z = zt[t % NXB]
                ztk = f"xtok{t % NXB}"
                hs = slice(half * 512, (half + 1) * 512)
                bank, btoks = proj_bank(t, half)
                _stt(P, z[:, hs], xtok[t % NXB][:, hs], ALPHA, bank, ALU.mult, ALU.add, btoks[:1] + [ztk], [ztk])

            def tile1_post(t):
                z = zt[t % NXB]
                ztk = f"xtok{t % NXB}"
                m = mv[t % NXB]
                mt = f"mv{t % NXB}"
                P.op("dve", lambda e: e.bn_stats(out=bst[:, 0:6], in_=z[:, 0:512]), r=[ztk, "bst"], w=["bst"])
                P.op("dve", lambda e: e.bn_stats(out=bst[:, 6:12], in_=z[:, 512:1024]), r=[ztk, "bst"], w=["bst"])
                P.op("dve", lambda e: e.bn_aggr(out=m[:, 0:2], in_=bst[:]), r=["bst", mt], w=[mt])
                _ts(P, "dve", m[:, 2:3], m[:, 1:2], LN_EPS, ALU.add, [mt], [mt])

            def tile2(t, flush=False):
                z = zt[t % NXB]
                ztk = f"xtok{t % NXB}"
                m = mv[t % NXB]
                mt = f"mv{t % NXB}"
                _act(P, m[:, 2:3], m[:, 2:3], AF.Ln, [mt], [mt])
                _act(P, m[:, 3:4], m[:, 2:3], AF.Exp, [mt], [mt], scale=-0.5)
                _ts(P, "dve", z[:], z[:], m[:, 0:1], ALU.subtract, [ztk, mt], [ztk], s2=m[:, 3:4], op1=ALU.mult)
                _tt(P, "dve" if flush else "pool", z[:], z[:], gain_b[:], ALU.mult, [ztk, "gain"], [ztk])
                _tt(P, "pool", z[:], z[:], bias_b[:], ALU.add, [ztk, "lbias"], [ztk])
                _dma(P, "sp", dap(out_d, t * 128 * D, [[D, 128], [1, D]]), z[:], "out", [ztk], [f"out{t}"])
                if t + NXB < NT:
                    load_x(t + NXB)

            its = []
            n = 0
            for g in range(NG):
                for c in range(4):
                    for kt in range(4 * g + 4):
                        its.append((g, c, kt, n))
                        n += 1
            load_qs(0)
            load_sga(0, 0)
            load_sga(0, 1)
            load_mixp(0)
            for t0 in range(NXB):
                load_x(t0)
            pending = []
            deferred = []
            posted = []
            t2_done = set()
            state = {"cool": 0, "idx": 0}
            post_iter = {}

            def do_tile2(t, flush=False):
                if t not in t2_done:
                    tile2(t, flush)
                    t2_done.add(t)

            def push_tile(t):
                for half in range(2):
                    for cc in range(8):
                        pending.append(("mm", t, half, cc))
                    pending.append(("half", t, half))
                pending.append(("fin", t))

            def pop_one(flush=False):
                item = pending.pop(0)
                if item[0] == "mm":
                    tile1_mm(item[1], item[2], item[3])
                    return False
                t = item[1]
                if item[0] == "half":
                    if item[2] == 0 and t >= NXB:
                        do_tile2(t - NXB, flush)
                    tile1_half(t, item[2])
                    state["cool"] = 1
                    return True
                tile1_post(t)
                posted.append(t)
                post_iter[t] = state["idx"]
                if t % 4 == 3 and t // 4 + 2 < NG:
                    load_mixp(t // 4 + 2)
                state["cool"] = 0
                return True

            def flush_groups_upto(gmax):
                while pending and pending[0][1] // 4 <= gmax:
                    pop_one()

            QK(its[0])
            QK(its[1])
            for idx, it in enumerate(its):
                g, c, kt, n = it
                state["idx"] = idx
                if c == 0 and kt == 0 and g + 1 < NG:
                    load_qs(g + 1)
                if idx + 2 < len(its):
                    QK(its[idx + 2])
                last = kt == 4 * g + 3
                if last:
                    flush_groups_upto(g - 2)
                burst = state.pop("burst", 0)
                while burst > 0 and pending:
                    burst -= 1
                    if pop_one():
                        break
                PV(it)
                while deferred and deferred[0][0] <= n:
                    deferred.pop(0)[1]()
                if state["cool"] > 0:
                    state["cool"] -= 1
                elif pending:
                    k = 2 if len(pending) > 22 else 1
                    for _ in range(k):
                        if not pending or pop_one():
                            break
                if last:
                    b = 4 * g + c
                    if b - 4 >= 0:
                        push_tile(b - 4)
                    state["burst"] = 8
                    for t in list(posted):
                        if idx - post_iter[t] >= 3:
                            do_tile2(t)
                    if b == 3 and NG > 1:
                        load_mixp(1)
            while deferred:
                deferred.pop(0)[1]()
            while pending:
                pop_one(flush=True)
            for t in list(posted):
                do_tile2(t, flush=True)
            state["tail"] = True
            for t in range(NT - 4, NT):
                push_tile(t)
                while pending:
                    pop_one(flush=True)
            for t in range(NT):
                do_tile2(t, flush=True)
            _run_block(nc, P, _sem_names(P), g_es, "b")
    return nc


_NC_CACHE = {}


def kernel(x, positions, w_in, pool_w, pool_scale, w_out, ln_gain, ln_bias):
    x = np.asarray(x, dtype=np.float32)
    positions = np.asarray(positions, dtype=np.int32)
    w_in = np.ascontiguousarray(np.asarray(w_in, dtype=np.float32)[0])
    pool_w = np.ascontiguousarray(np.asarray(pool_w, dtype=np.float32)[0])
    pool_scale = np.ascontiguousarray(np.asarray(pool_scale, dtype=np.float32)[0].reshape(4, 128).T)
    w_out = np.ascontiguousarray(np.asarray(w_out, dtype=np.float32)[0])
    ln_gain = np.ascontiguousarray(np.asarray(ln_gain, dtype=np.float32)[0].reshape(1, D))
    ln_bias = np.ascontiguousarray(np.asarray(ln_bias, dtype=np.float32)[0].reshape(1, D))
    if "nc" not in _NC_CACHE:
        _NC_CACHE["nc"] = build_nc()
    nc = _NC_CACHE["nc"]
    in_maps = []
    for b in range(8):
        xb = np.ascontiguousarray(x[b])
        in_maps.append({
            "xT": np.ascontiguousarray(xb.T),
            "x": xb,
            "pos": np.ascontiguousarray(positions[b].reshape(NT, 128).T),
            "w_in": w_in, "pool_w": pool_w, "pool_scale": pool_scale, "w_out": w_out,
            "ln_gain": ln_gain, "ln_bias": ln_bias,
        })
    res = run_bass_kernel_spmd(nc, in_maps, core_ids=list(range(8)))
    if DEBUG:
        kernel.last = res
    return np.stack([np.asarray(r["out"], dtype=np.float32) for r in res.results], axis=0)
```

```python
import math
from contextlib import ExitStack

import numpy as np
import concourse.bass as bass
import concourse.mybir as mybir
from concourse.bass_utils import run_bass_kernel_spmd

F32 = mybir.dt.float32
BF16 = mybir.dt.bfloat16
I32 = mybir.dt.int32
AF = mybir.ActivationFunctionType
ALU = mybir.AluOpType
AX = mybir.AxisListType

S = 4096
D = 1024
NT = S // 128
NG = S // 512
NEG = -30000.0
ALPHA = 2.0 ** 0.25
LN_EPS = 1e-5
DEBUG = False


class Prog:
    def __init__(self):
        self.ops = []

    def op(self, eng, fn, r=(), w=(), dma=None):
        self.ops.append(dict(eng=eng, fn=fn, r=tuple(r), w=tuple(w), dma=dma,
                             deps=set(), sig=dma is not None))
        return len(self.ops) - 1

    def analyze(self):
        last_w = {}
        readers = {}
        for i, o in enumerate(self.ops):
            deps = set()
            for t in o["r"]:
                if t in last_w:
                    deps.add(last_w[t])
            for t in o["w"]:
                if t in last_w:
                    deps.add(last_w[t])
                for j in readers.get(t, ()):
                    deps.add(j)
            deps.discard(i)
            for t in o["w"]:
                last_w[t] = i
                readers[t] = []
            for t in o["r"]:
                if t not in o["w"]:
                    readers.setdefault(t, []).append(i)
            keep = set()
            for j in deps:
                p = self.ops[j]
                if p["dma"] is None and p["eng"] == "pe" and o["eng"] == "pe" and o["dma"] is None:
                    continue
                keep.add(j)
            o["deps"] = keep
            for j in keep:
                self.ops[j]["sig"] = True

    def emit(self, sems):
        self.analyze()
        cnt = {}
        for o in self.ops:
            key = o["dma"] if o["dma"] is not None else o["eng"]
            o["key"] = key
            if o["sig"]:
                cnt[key] = cnt.get(key, 0) + 1
                o["count"] = cnt[key] * (16 if o["dma"] is not None else 1)
        streams = {}
        for i, o in enumerate(self.ops):
            streams.setdefault(o["eng"], []).append(i)
        self.total = dict(cnt)

        def run_stream(engname, eng):
            waited = {}
            for i in streams.get(engname, []):
                o = self.ops[i]
                need = {}
                for j in o["deps"]:
                    p = self.ops[j]
                    k = p["key"]
                    need[k] = max(need.get(k, 0), p["count"])
                for k, v in need.items():
                    if waited.get(k, 0) < v:
                        eng.wait_ge(sems[k], v)
                        waited[k] = v
                ins = o["fn"](eng)
                if o["sig"]:
                    ins.then_inc(sems[o["key"]], 16 if o["dma"] is not None else 1)

        return run_stream


def _mm(P, out, lhsT, rhs, start, stop, r, w, tp=None):
    def fn(e):
        if tp is None:
            return e.matmul(out, lhsT=lhsT, rhs=rhs, start=start, stop=stop)
        return e.matmul(out, lhsT=lhsT, rhs=rhs, start=start, stop=stop, tile_position=tp)
    P.op("pe", fn, r=r, w=w)


def _tr(P, out, in_, ident, r, w):
    P.op("pe", lambda e: e.transpose(out=out, in_=in_, identity=ident), r=r, w=w)


def _act(P, out, in_, func, r, w, scale=None):
    def fn(e):
        if scale is None:
            return e.activation(out=out, in_=in_, func=func)
        return e.activation(out=out, in_=in_, func=func, scale=scale)
    P.op("act", fn, r=r, w=w)


def _copy(P, eng, out, in_, r, w):
    if eng == "act":
        _act(P, out, in_, AF.Copy, r, w)
    else:
        P.op(eng, lambda e: e.tensor_copy(out=out, in_=in_), r=r, w=w)


def _tt(P, eng, out, in0, in1, op, r, w):
    P.op(eng, lambda e: e.tensor_tensor(out=out, in0=in0, in1=in1, op=op), r=r, w=w)


def _ts(P, eng, out, in0, s1, op0, r, w, s2=None, op1=None):
    def fn(e):
        if op1 is None:
            return e.tensor_scalar(out=out, in0=in0, scalar1=s1, scalar2=None, op0=op0)
        return e.tensor_scalar(out=out, in0=in0, scalar1=s1, scalar2=s2, op0=op0, op1=op1)
    P.op(eng, fn, r=r, w=w)


def _stt(P, out, in0, scalar, in1, op0, op1, r, w):
    P.op("dve", lambda e: e.scalar_tensor_tensor(out=out, in0=in0, scalar=scalar, in1=in1, op0=op0, op1=op1),
         r=r, w=w)


def _memset(P, eng, ap, val, r, w):
    P.op(eng, lambda e: e.memset(ap, val), r=r, w=w)


def _dma(P, eng, out, in_, stream, r, w):
    key = ("L:" + w[0]) if not r else ("S:" + r[0])
    P.op(eng, lambda e: e.dma_start(out=out, in_=in_), r=r, w=w, dma=key)


def _run_block(nc, P, sem_names, es, tag):
    sems = {n: es.enter_context(nc.semaphore(tag + "_" + n.replace(":", "_"))) for n in sem_names}
    run = P.emit(sems)
    with nc.Block() as block:
        @block.sync
        def _(e):
            run("sp", e)
            for k, v in P.total.items():
                if k not in ("pe", "act", "dve", "pool"):
                    e.wait_ge(sems[k], 16 * v)

        @block.tensor
        def _(e):
            run("pe", e)

        @block.scalar
        def _(e):
            run("act", e)

        @block.vector
        def _(e):
            run("dve", e)

        @block.gpsimd
        def _(e):
            run("pool", e)


def _sem_names(P):
    names = {"pe", "act", "dve", "pool"}
    for o in P.ops:
        if o["dma"] is not None:
            names.add(o["dma"])
    return sorted(names)


def build_nc():
    nc = bass.Bass("TRN2", target_bir_lowering=False)
    xT_d = nc.dram_tensor("xT", [D, S], F32, kind="ExternalInput")
    x_d = nc.dram_tensor("x", [S, D], F32, kind="ExternalInput")
    pos_d = nc.dram_tensor("pos", [128, NT], I32, kind="ExternalInput")
    win_d = nc.dram_tensor("w_in", [D, 3072], F32, kind="ExternalInput")
    poolw_d = nc.dram_tensor("pool_w", [4, 128, 128], F32, kind="ExternalInput")
    pscale_d = nc.dram_tensor("pool_scale", [128, 4], F32, kind="ExternalInput")
    wout_d = nc.dram_tensor("w_out", [D, D], F32, kind="ExternalInput")
    gain_d = nc.dram_tensor("ln_gain", [1, D], F32, kind="ExternalInput")
    lbias_d = nc.dram_tensor("ln_bias", [1, D], F32, kind="ExternalInput")
    out_d = nc.dram_tensor("out", [S, D], F32, kind="ExternalOutput")
    skind = "ExternalOutput" if DEBUG else "Internal"
    qsp_d = nc.dram_tensor("q_sp", [8, 80, S], BF16, kind=skind)
    gsp_d = nc.dram_tensor("g_sp", [512, S], F32, kind=skind)
    mpsp_d = nc.dram_tensor("mp_sp", [512, S], BF16, kind=skind)
    if DEBUG:
        kdbg_d = nc.dram_tensor("k_dbg", [80, 8 * S], BF16, kind="ExternalOutput")
        vdbg_d = nc.dram_tensor("v_dbg", [128, NT * 8 * 96], BF16, kind="ExternalOutput")

    def dap(t, off, pat):
        return bass.AP(t, off, pat)

    with ExitStack() as g_es:
        def gsb(name, shape, dt):
            return g_es.enter_context(nc.sbuf_tensor(name, shape, dt))
        kaug = gsb("kaug", [128, 8, S], BF16)
        vaug = gsb("vaug", [128, NT, 8, 96], BF16)
        wbuf = gsb("wbuf", [128, 8, 1536], BF16)
        xg = gsb("xg", [128, 4 * D], F32)
        xTbf_g = [xg[:, j * 2048:(j + 1) * 2048].bitcast(BF16).rearrange("p (k n) -> p k n", k=8) for j in range(2)]
        ident = gsb("ident", [128, 128], BF16)
        tribias = gsb("tribias", [128, 128], BF16)
        ones_bf = gsb("ones_bf", [128, 128], BF16)
        zeros_bf = gsb("zeros_bf", [128, 128], BF16)
        cosT = gsb("cosT", [128, NT, 8], F32)
        sinT = gsb("sinT", [128, NT, 8], F32)

        with ExitStack() as es:
            def sb(name, shape, dt):
                return es.enter_context(nc.sbuf_tensor(name, shape, dt))

            def ps(name, shape, dt=F32):
                return es.enter_context(nc.psum_tensor(name, shape, dt))
            P = Prog()
            w_fm = wbuf
            xTbf = xTbf_g
            poolw_bf = sb("poolw_bf", [128, 4, 128], BF16)
            pscale = sb("pscale", [128, 4], F32)
            ubuf = sb("ubuf", [128, 4, 528], F32)
            Sa = sb("Sa", [128, 528], F32)
            Sb = sb("Sb", [128, 528], F32)
            d_bf = sb("d_bf", [128, 4, 512], BF16)
            sg = [sb(f"sg{i}", [128, 512], F32) for i in range(2)]
            mp_stage = [sb(f"mp_stage{i}", [128, 4, 512], BF16) for i in range(2)]
            sga_stage = [sb(f"sga_stage{i}", [128, 512], F32) for i in range(2)]
            rcw = sb("rcw", [128, 16], F32)
            fix = sb("fix", [128, 16], F32)
            posi = sb("posi", [128, NT], I32)
            posf = sb("posf", [128, NT], F32)
            ang = sb("ang", [128, NT * 8], F32)
            angc = sb("angc", [128, NT * 8], F32)
            yy = sb("yy", [128, NT * 8], F32)
            ki = sb("ki", [128, NT * 8], I32)
            kf = sb("kf", [128, NT * 8], F32)
            mk = sb("mk", [128, NT * 8], F32)
            pj = [ps(f"pjA{i}", [128, 512]) for i in range(6)]
            yp = [ps(f"ypA{i}", [128, 512]) for i in range(2)]

            def xload_a1(g):
                _dma(P, "pool", xTbf[g % 2], dap(xT_d, g * 512, [[S, 128], [128 * S, 8], [1, 512]]), "xld",
                     [], [f"xTA{g % 2}"])
            xload_a1(0)
            for k0, k1 in ((0, 4), (4, 8)):
                nk = k1 - k0
                _dma(P, "pool", w_fm[:, k0:k1, 0:1024], dap(win_d, k0 * 128 * 3072, [[3072, 128], [128 * 3072, nk], [1, 1024]]),
                     "wld", [], [f"w_fm{kc}a" for kc in range(k0, k1)])
            for k0, k1 in ((0, 4), (4, 8)):
                nk = k1 - k0
                _dma(P, "pool", w_fm[:, k0:k1, 1024:1536], dap(win_d, k0 * 128 * 3072 + 2560, [[3072, 128], [128 * 3072, nk], [1, 512]]),
                     "wld", [], [f"w_fm{kc}b" for kc in range(k0, k1)])
            _dma(P, "pool", poolw_bf[:], dap(poolw_d, 0, [[128, 128], [128 * 128, 4], [1, 128]]), "wld", [], ["poolw"])
            _dma(P, "sp", pscale[:], pscale_d.ap(), "misc", [], ["pscale"])
            _dma(P, "sp", posi[:], pos_d.ap(), "misc", [], ["posi"])

            _memset(P, "pool", ones_bf[:], 1.0, [], ["ones"])
            _memset(P, "pool", zeros_bf[:], 0.0, [], ["zeros"])
            P.op("pool", lambda e: e.affine_select(out=ident[:], in_=ones_bf[:], pattern=[[-1, 128]],
                                                   compare_op=ALU.is_equal, fill=0.0, base=0, channel_multiplier=1),
                 r=["ones"], w=["ident"])
            P.op("pool", lambda e: e.affine_select(out=tribias[:], in_=zeros_bf[:], pattern=[[1, 128]],
                                                   compare_op=ALU.is_ge, fill=NEG, base=0, channel_multiplier=-1),
                 r=["zeros"], w=["tribias"])
            for t in range(15):
                _memset(P, "pool", rcw[:, t:t + 1], 1.0 / (t + 1), ["rcw"], ["rcw"])
            _memset(P, "pool", rcw[:, 15:16], 1.0 / 16, ["rcw"], ["rcw"])
            freqs = (np.float32(500000.0) ** (-(np.arange(8, dtype=np.float32) * np.float32(2.0)) / np.float32(16.0))).astype(np.float32)
            _copy(P, "dve", posf[:], posi[:], ["posi"], ["posf"])
            ang3 = ang[:].rearrange("p (t j) -> p t j", j=8)
            for j in range(8):
                _ts(P, "dve", ang3[:, :, j], posf[:], float(freqs[j]), ALU.mult, ["posf", "ang"], ["ang"])
            _ts(P, "dve", angc[:], ang[:], math.pi / 2, ALU.add, ["ang"], ["angc"])
            C1 = 6.28125
            C2 = 2 * math.pi - 6.28125

            def sin_of(src, srctok, dst):
                _ts(P, "dve", yy[:], src[:], 1.0 / (2 * math.pi), ALU.mult, [srctok, "yy"], ["yy"])
                _copy(P, "dve", ki[:], yy[:], ["yy", "ki"], ["ki"])
                _copy(P, "dve", kf[:], ki[:], ["ki", "kf"], ["kf"])
                _stt(P, src[:], kf[:], -C1, src[:], ALU.mult, ALU.add, ["kf", srctok], [srctok])
                _stt(P, src[:], kf[:], -C2, src[:], ALU.mult, ALU.add, ["kf", srctok], [srctok])
                _ts(P, "dve", mk[:], src[:], math.pi, ALU.is_gt, [srctok, "mk"], ["mk"])
                _stt(P, src[:], mk[:], -2 * math.pi, src[:], ALU.mult, ALU.add, ["mk", srctok], [srctok])
                _ts(P, "dve", src[:], src[:], -math.pi, ALU.max, [srctok], [srctok], s2=math.pi, op1=ALU.min)
                _act(P, dst[:].rearrange("p t j -> p (t j)"), src[:], AF.Sin, [srctok], ["rot_tab"])
            sin_of(ang, "ang", sinT)
            sin_of(angc, "angc", cosT)

            chunk_seq = [("u", 0), ("gp", 0), ("u", 1), ("gp", 1), ("u", 2), ("gp", 2), ("u", 3), ("gp", 3),
                         ("ga", 0), ("ga", 1), ("ga", 2), ("ga", 3)]
            nbank = 0

            for g in range(NG):
                xb = g % 2
                if g + 1 < NG:
                    xload_a1(g + 1)
                else:
                    _dma(P, "pool", xTbf[0], dap(xT_d, 0, [[S, 128], [128 * S, 8], [1, 512]]), "xld", [], ["xTA0"])
                pend_pool_mm = []

                def pool_mm(gi, g=g, xb=xb):
                    _mm(P, yp[gi % 2][:], poolw_bf[:, gi, :], d_bf[:, gi, :], True, True,
                        ["poolw", f"d{gi}"], [f"yp{gi % 2}"])
                    _stt(P, mp_stage[xb][:, gi, :], yp[gi % 2][:], pscale[:, gi:gi + 1], sg[gi % 2][:],
                         ALU.mult, ALU.mult, [f"yp{gi % 2}", "pscale", f"sg{gi % 2}"], [f"mp{xb}_{gi}"])

                for kind, idx in chunk_seq:
                    bank = pj[nbank % 6]
                    btok = f"pj{nbank % 6}"
                    nbank += 1
                    col0 = {"u": idx * 128, "gp": 512 + idx * 128, "ga": 1024 + idx * 128}[kind]
                    for kc in range(8):
                        _mm(P, bank[:], w_fm[:, kc, col0:col0 + 128], xTbf[xb][:, kc, :], kc == 0, kc == 7,
                            [f"w_fm{kc}b" if kind == "ga" else f"w_fm{kc}a", f"xTA{xb}"], [btok])
                    if kind == "u":
                        gi = idx
                        w = 2 << gi
                        ut = f"ubuf{gi}"
                        if len(pend_pool_mm) >= 2:
                            pool_mm(pend_pool_mm.pop(0))
                        if g == 0:
                            _memset(P, "pool", ubuf[:, gi, 0:16], 0.0, [ut], [ut])
                        else:
                            _copy(P, "pool", ubuf[:, gi, 0:16], ubuf[:, gi, 512:528], [ut], [ut])
                        _copy(P, "act", ubuf[:, gi, 16:528], bank[:], [btok, ut], [ut])
                        _tt(P, "pool", Sa[:, 1:528], ubuf[:, gi, 1:528], ubuf[:, gi, 0:527], ALU.add, [ut, "Sa"], ["Sa"])
                        cur, curt = Sa, "Sa"
                        if w >= 4:
                            _tt(P, "pool", Sb[:, 3:528], Sa[:, 3:528], Sa[:, 1:526], ALU.add, ["Sa", "Sb"], ["Sb"])
                            cur, curt = Sb, "Sb"
                        if w >= 8:
                            _tt(P, "pool", Sa[:, 7:528], Sb[:, 7:528], Sb[:, 3:524], ALU.add, ["Sb", "Sa"], ["Sa"])
                            cur, curt = Sa, "Sa"
                        if w >= 16:
                            _tt(P, "pool", Sb[:, 15:528], Sa[:, 15:528], Sa[:, 7:520], ALU.add, ["Sa", "Sb"], ["Sb"])
                            cur, curt = Sb, "Sb"
                        _stt(P, d_bf[:, gi, :], cur[:, 16:528], 1.0 / w, ubuf[:, gi, 16:528], ALU.mult, ALU.subtract,
                             [curt, ut, f"d{gi}"], [f"d{gi}"])
                        if g == 0:
                            _tt(P, "dve", fix[:, 0:w - 1], cur[:, 16:16 + w - 1], rcw[:, 0:w - 1], ALU.mult,
                                [curt, "rcw", "fix"], ["fix"])
                            _tt(P, "dve", d_bf[:, gi, 0:w - 1], fix[:, 0:w - 1], ubuf[:, gi, 16:16 + w - 1], ALU.subtract,
                                ["fix", ut, f"d{gi}"], [f"d{gi}"])
                        pend_pool_mm.append(gi)
                    elif kind == "gp":
                        gi = idx
                        _act(P, sg[gi % 2][:], bank[:], AF.Silu, [btok, f"sg{gi % 2}"], [f"sg{gi % 2}"])
                        if g == NG - 1 and gi == 3:
                            _dma(P, "pool", wbuf[:, :, 0:1024], dap(win_d, 1024, [[3072, 128], [128 * 3072, 8], [1, 1024]]),
                                 "wld", [], [f"w_fm{kc}a" for kc in range(8)])
                    else:
                        j = idx
                        st = sga_stage[j % 2]
                        _act(P, st[:], bank[:], AF.Silu, [btok, f"sga{j % 2}"], [f"sga{j % 2}"])
                        _dma(P, "sp", dap(gsp_d, (j * 128) * S + g * 512, [[S, 128], [1, 512]]), st[:], "spw",
                             [f"sga{j % 2}"], [f"gsp{g}"])
                        if pend_pool_mm and j in (0, 2):
                            pool_mm(pend_pool_mm.pop(0))
                while pend_pool_mm:
                    pool_mm(pend_pool_mm.pop(0))
                _dma(P, "sp", dap(mpsp_d, g * 512, [[S, 128], [128 * S, 4], [1, 512]]), mp_stage[xb][:], "spw",
                     [f"mp{xb}_{i}" for i in range(4)], [f"mpsp{g}"])
            _dma(P, "pool", wbuf[:, :, 1024:1536], dap(win_d, 2048, [[3072, 128], [128 * 3072, 8], [1, 512]]),
                 "wld", [], [f"w_fm{kc}b" for kc in range(8)])
            _run_block(nc, P, _sem_names(P), g_es, "a1")

        with ExitStack() as es:
            def sb(name, shape, dt):
                return es.enter_context(nc.sbuf_tensor(name, shape, dt))

            def ps(name, shape, dt=F32):
                return es.enter_context(nc.psum_tensor(name, shape, dt))
            P = Prog()
            w_tm = wbuf
            xTbf = xTbf_g
            NQK = 4
            qk_tok = [sb(f"qk_tok{i}", [128, 16, 80], BF16) for i in range(NQK)]
            rt = [sb(f"rt{i}", [128, 16, 8], F32) for i in range(4)]
            qTg = [sb(f"qTg{i}", [128, 8, 128], BF16) for i in range(2)]
            gm = sb("gm", [128, 8, 16], F32)
            top8 = sb("top8", [128, 8, 8], F32)
            sel = sb("sel", [128, 8, 16], F32)
            ksum = sb("ksum", [128, 8], F32)
            kmT = sb("kmT", [128, 8, 16], BF16)
            qstage = [sb(f"qstage{i}", [128, 8, 512], BF16) for i in range(2)]
            qb = ps("qbank", [128, 512])
            kb = ps("kbank", [128, 512])
            vb = ps("vbank", [128, 512])
            NTR = 4
            tr = [ps(f"trb{i}", [128, 1024], BF16) for i in range(NTR)]
            gt = ps("gtbank", [128, 512])

            _memset(P, "dve", gm[:], -1.0e30, [], ["gm"])
            for t8 in range(0, NT, 8):
                _memset(P, "pool", vaug[:, t8:t8 + 8, :, 64:96], 1.0, [], ["vones"])
            ntr_box = [0]
            trk_of = {}
            trq_of = {}

            def next_tr():
                i = ntr_box[0] % NTR
                ntr_box[0] += 1
                return tr[i], f"tr{i}"

            def xload_a2(g):
                _dma(P, "pool", xTbf[g % 2], dap(xT_d, g * 512, [[S, 128], [128 * S, 8], [1, 512]]), "xld",
                     [], [f"xTB{g % 2}"])

            def S12(t):
                g, tt = divmod(t, 4)
                xb = g % 2
                cur = t // 2
                qk = qk_tok[t % NQK]
                qkt = f"qk{t % NQK}"
                for ci, (bank, btok) in enumerate(((qb, "qb"), (kb, "kb"), (vb, "vb"))):
                    for kc in range(8):
                        _mm(P, bank[:], xTbf[xb][:, kc, tt * 128:(tt + 1) * 128], w_tm[:, kc, ci * 512:(ci + 1) * 512],
                            kc == 0, kc == 7, [f"xTB{xb}", f"w_tm{kc}"], [btok])
                cosb = cosT[:, t:t + 1, :].to_broadcast([128, 8, 8])
                sinb = sinT[:, t:t + 1, :].to_broadcast([128, 8, 8])
                for bank, btok, h0 in ((qb, "qb", 0), (kb, "kb", 8)):
                    X = bank[:].rearrange("p (h d) -> p h d", d=64)
                    x1 = X[:, :, 0:8]
                    x2 = X[:, :, 8:16]
                    hs = slice(h0, h0 + 8)
                    _tt(P, "dve", rt[0][:, hs, :], x1, cosb, ALU.mult, [btok, "rot_tab", "rt0"], ["rt0"])
                    _tt(P, "dve", rt[1][:, hs, :], x2, sinb, ALU.mult, [btok, "rot_tab", "rt1"], ["rt1"])
                    _tt(P, "dve", rt[2][:, hs, :], x2, cosb, ALU.mult, [btok, "rot_tab", "rt2"], ["rt2"])
                    _tt(P, "dve", rt[3][:, hs, :], x1, sinb, ALU.mult, [btok, "rot_tab", "rt3"], ["rt3"])
                    _tt(P, "dve", qk[:, hs, 0:8], rt[0][:, hs, :], rt[1][:, hs, :], ALU.subtract, ["rt0", "rt1", qkt], [qkt])
                    _tt(P, "dve", qk[:, hs, 8:16], rt[2][:, hs, :], rt[3][:, hs, :], ALU.add, ["rt2", "rt3", qkt], [qkt])
                    _copy(P, "act", qk[:, hs, 16:64], X[:, :, 16:64], [btok, qkt], [qkt])
                _copy(P, "act", vaug[:, t, :, 0:64], vb[:].rearrange("p (h d) -> p h d", d=64), ["vb"], [f"v{t}"])
                _memset(P, "pool", qk[:, 8:16, 64:80], 0.0, [qkt], [qkt])
                _memset(P, "pool", qk[:, 8:16, 64 + cur:65 + cur], 1.0, [qkt], [qkt])
                if cur == 0:
                    _memset(P, "pool", qk[:, 0:8, 64:80], 0.0, [qkt], [qkt])

            def S34(t):
                cur = t // 2
                qk = qk_tok[t % NQK]
                qkt = f"qk{t % NQK}"
                trk, trkt = next_tr()
                for h in range(8):
                    _tr(P, trk[0:80, h * 128:(h + 1) * 128], qk[:, 8 + h, 0:80], ident[:], [qkt, "ident"], [trkt])
                if cur > 0:
                    trq, trqt = next_tr()
                    for h in range(8):
                        _tr(P, trq[0:64, h * 128:(h + 1) * 128], qk[:, h, 0:64], ident[:], [qkt, "ident"], [trqt])
                _copy(P, "act", kaug[0:80, :, t * 128:(t + 1) * 128], trk[0:80, :].rearrange("p (h n) -> p h n", h=8),
                      [trkt], [f"kaug{t}"])
                if cur > 0:
                    _copy(P, "dve", qTg[t % 2][0:64, :, :], trq[0:64, :].rearrange("p (h n) -> p h n", h=8),
                          [trqt, f"qTg{t % 2}"], [f"qTg{t % 2}"])
                for h in range(8):
                    _mm(P, gt[0:64, 128 + h:129 + h], qk[:, 8 + h, 0:64], ones_bf[:, 0:1], True, True, [qkt], ["gt"])
                if t % 2 == 0:
                    _copy(P, "act", ksum[0:64, :], gt[0:64, 128:136], ["gt", "ksum"], ["ksum"])
                else:
                    _tt(P, "dve", kmT[0:64, :, cur], ksum[0:64, :], gt[0:64, 128:136], ALU.add, ["ksum", "gt"], [f"km{cur}"])

            def S56(t):
                cur = t // 2
                if cur == 0:
                    return
                qk = qk_tok[t % NQK]
                qkt = f"qk{t % NQK}"
                for h in range(8):
                    _mm(P, gt[:, h * 16:h * 16 + cur], qTg[t % 2][0:64, h, :], kmT[0:64, h, 0:cur], True, True,
                        [f"qTg{t % 2}"] + [f"km{n}" for n in range(cur)], ["gt"])
                gt3 = gt[:, 0:128].rearrange("p (h n) -> p h n", n=16)
                _copy(P, "dve", gm[:, :, 0:cur], gt3[:, :, 0:cur], ["gt", "gm"], ["gm"])
                for h in range(8):
                    P.op("dve", (lambda h: lambda e: e.max(out=top8[:, h, :], in_=gm[:, h, :]))(h),
                         r=["gm", "top8"], w=["top8"])
                _tt(P, "dve", sel[:], gm[:], top8[:, :, 2:3].to_broadcast([128, 8, 16]), ALU.is_ge,
                    ["gm", "top8", "sel"], ["sel"])
                _ts(P, "dve", qk[:, 0:8, 64:80], sel[:], -1.0, ALU.add, ["sel", qkt], [qkt], s2=-NEG, op1=ALU.mult)
                _memset(P, "pool", qk[:, 0:8, 64 + cur:65 + cur], 0.0, [qkt], [qkt])

            def S78(t):
                g, tt = divmod(t, 4)
                xb = g % 2
                qk = qk_tok[t % NQK]
                qkt = f"qk{t % NQK}"
                trf, trft = next_tr()
                for h in range(8):
                    _tr(P, trf[0:80, h * 128:(h + 1) * 128], qk[:, h, 0:80], ident[:], [qkt, "ident"], [trft])
                _copy(P, "act", qstage[xb][0:80, :, tt * 128:(tt + 1) * 128],
                      trf[0:80, :].rearrange("p (h n) -> p h n", h=8), [trft, f"qst{xb}"], [f"qst{xb}"])
                if tt == 3:
                    _dma(P, "sp", dap(qsp_d, g * 512, [[S, 80], [80 * S, 8], [1, 512]]), qstage[xb][0:80, :, :], "spw",
                         [f"qst{xb}"], [f"qsp{g}"])

            for i in range(NT + 3):
                if i % 4 == 0 and i // 4 + 1 < NG and i < NT:
                    xload_a2(i // 4 + 1)
                if 0 <= i - 3 < NT:
                    S78(i - 3)
                if 0 <= i - 2 < NT:
                    S56(i - 2)
                if 0 <= i - 1 < NT:
                    S34(i - 1)
                if i < NT:
                    S12(i)
            if DEBUG:
                _dma(P, "sp", kdbg_d.ap(), kaug[0:80, :, :].rearrange("p h n -> p (h n)"), "spw",
                     [f"kaug{t}" for t in range(NT)], ["kdbg"])
                _dma(P, "sp", vdbg_d.ap(), vaug[:].rearrange("p t h n -> p (t h n)"), "spw",
                     [f"v{t}" for t in range(NT)], ["vdbg"])
            _run_block(nc, P, _sem_names(P), g_es, "a2")

        with ExitStack() as es:
            def sb(name, shape, dt):
                return es.enter_context(nc.sbuf_tensor(name, shape, dt))

            def ps(name, shape, dt=F32):
                return es.enter_context(nc.psum_tensor(name, shape, dt))
            P = Prog()
            wout_bf = wbuf[:, :, 0:D]
            gain_b = sb("gain_b", [128, D], F32)
            bias_b = sb("bias_b", [128, D], F32)
            qaug = [sb(f"qaug{i}", [128, 8, 512], BF16) for i in range(2)]
            mixT = [sb(f"mixT{i}", [128, 8, 512], BF16) for i in range(2)]
            sga = [sb(f"sga{i}", [128, 512], F32) for i in range(2)]
            PT = [sb(f"PT{i}", [128, 2, 512], BF16) for i in range(3)]
            ontok = sb("ontok", [128, 4, 128], BF16)
            rdn = sb("rdn", [128, 8], F32)
            NXB = 4
            xtok = [xg[:, i * D:(i + 1) * D] for i in range(NXB)]
            zt = xtok
            bst = sb("bst", [128, 12], F32)
            mv = [sb(f"mv{i}", [128, 4], F32) for i in range(NXB)]
            scp = [ps(f"scp{i}", [128, 2, 512]) for i in range(2)]
            accT = [ps(f"accT{i}", [128, 512]) for i in range(2)]
            opb1 = ps("opb1", [128, 512])
            trp = ps("trp", [128, 1024], BF16)

            _dma(P, "pool", wout_bf, dap(wout_d, 0, [[D, 128], [128 * D, 8], [1, D]]), "wld", [], ["wout"])
            _dma(P, "sp", gain_b[:], dap(gain_d, 0, [[0, 128], [1, D]]), "misc", [], ["gain"])
            _dma(P, "sp", bias_b[:], dap(lbias_d, 0, [[0, 128], [1, D]]), "misc", [], ["lbias"])

            def load_qs(g):
                gb = g % 2
                _dma(P, "sp", qaug[gb][0:80, :, :], dap(qsp_d, g * 512, [[S, 80], [80 * S, 8], [1, 512]]), "qld",
                     [], [f"qaug{gb}"])

            def load_sga(g, c):
                k = (4 * g + c) % 2
                _dma(P, "sp", sga[k][:], dap(gsp_d, (c * 128) * S + g * 512, [[S, 128], [1, 512]]), "qld",
                     [], [f"sga{k}"])

            def load_mixp(g):
                gb = g % 2
                _dma(P, "sp", mixT[gb][:, 0:4, :], dap(mpsp_d, g * 512, [[S, 128], [128 * S, 4], [1, 512]]), "qld",
                     [], [f"mixp{gb}"])

            def load_x(t):
                _dma(P, "sp", xtok[t % NXB][:], dap(x_d, t * 128 * D, [[D, 128], [1, D]]), "xres", [], [f"xtok{t % NXB}"])

            def QK(it):
                g, c, kt, n = it
                gb = g % 2
                j = kt - 4 * g
                c0 = 0 if j < 0 else 128 * j
                sc = scp[n % 2]
                sct = f"scp{n % 2}"
                for hi, h in enumerate((2 * c, 2 * c + 1)):
                    _mm(P, sc[:, hi, c0:512], kaug[0:80, h, kt * 128:(kt + 1) * 128], qaug[gb][0:80, h, c0:512],
                        True, j < 0, [f"qaug{gb}"], [sct])
                    if j >= 0:
                        _mm(P, sc[:, hi, c0:c0 + 128], ident[:], tribias[:], False, True, [], [sct])
                pt = PT[n % 3]
                ptt = f"PT{n % 3}"
                _act(P, pt[:, :, c0:512], sc[:, :, c0:512], AF.Exp, [sct, ptt], [ptt], scale=0.125)

            def PV(it):
                g, c, kt, n = it
                gb = g % 2
                j = kt - 4 * g
                s0 = 0 if j < 0 else j
                A, B_ = 2 * c, 2 * c + 1
                pt = PT[n % 3]
                ptt = f"PT{n % 3}"
                first = kt == 0
                last = kt == 4 * g + 3
                for hi, h in enumerate((A, B_)):
                    for sq in range(s0, 4):
                        _mm(P, accT[hi][:, sq * 128:sq * 128 + 65], pt[:, hi, sq * 128:(sq + 1) * 128], vaug[:, kt, h, 0:65],
                            first and sq == 0, last and sq == 3, [ptt], [f"accT{hi}"])
                if last:
                    k = (4 * g + c) % 2
                    XA = accT[0][:].rearrange("p (s c) -> p s c", c=128)
                    XB = accT[1][:].rearrange("p (s c) -> p s c", c=128)
                    P.op("dve", lambda e: e.reciprocal(out=rdn[:, 0:4], in_=XA[:, :, 64]), r=["accT0", "rdn"], w=["rdn"])
                    P.op("dve", lambda e: e.reciprocal(out=rdn[:, 4:8], in_=XB[:, :, 64]), r=["accT1", "rdn"], w=["rdn"])
                    rA = rdn[:, 0:4].rearrange("p (s o) -> p s o", o=1).to_broadcast([128, 4, 64])
                    rB = rdn[:, 4:8].rearrange("p (s o) -> p s o", o=1).to_broadcast([128, 4, 64])
                    _tt(P, "dve", ontok[:, :, 0:64], XA[:, :, 0:64], rA, ALU.mult, ["accT0", "rdn", "ontok"], ["ontok"])
                    _tt(P, "dve", ontok[:, :, 64:128], XB[:, :, 0:64], rB, ALU.mult, ["accT1", "rdn", "ontok"], ["ontok"])

                    def back_to_feature_major(g=g, c=c, gb=gb, k=k):
                        for sq in range(4):
                            _tr(P, trp[:, sq * 128:(sq + 1) * 128], ontok[:, sq, :], ident[:], ["ontok", "ident"], ["trp"])
                        _tt(P, "dve", mixT[gb][:, 4 + c, :], trp[:, 0:512], sga[k][:], ALU.mult,
                            ["trp", f"sga{k}", f"mixa{gb}_{c}"], [f"mixa{gb}_{c}"])
                        nb = 4 * g + c + 2
                        if nb < 4 * NG:
                            load_sga(nb // 4, nb % 4)
                    deferred.append((n + 2, back_to_feature_major))

            def proj_bank(t, half):
                if state.get("tail"):
                    i = (2 * t + half) % 4
                    return scp[i // 2][:, i % 2, :], [f"fl{i}", f"scp{i // 2}"]
                return opb1[:], ["opb1"]

            def tile1_mm(t, half, cc):
                g, tt = divmod(t, 4)
                gb = g % 2
                rtoks = [f"mixp{gb}"] if cc < 4 else [f"mixa{gb}_{cc - 4}"]
                bank, btoks = proj_bank(t, half)
                _mm(P, bank, mixT[gb][:, cc, tt * 128:(tt + 1) * 128],
                    wout_bf[:, cc, half * 512:(half + 1) * 512], cc == 0, cc == 7,
                    rtoks + ["wout"], btoks)

            def tile1_half(t, half):
                z = zt[t % NXB]
                ztk = f"xtok{t % NXB}"
                hs = slice(half * 512, (half + 1) * 512)
                bank, btoks = proj_bank(t, half)
                _stt(P, z[:, hs], xtok[t % NXB][:, hs], ALPHA, bank, ALU.mult, ALU.add, btoks[:1] + [ztk], [ztk])

            def tile1_post(t):
                z = zt[t % NXB]
                ztk = f"xtok{t % NXB}"
                m = mv[t % NXB]
                mt = f"mv{t % NXB}"
                P.op("dve", lambda e: e.bn_stats(out=bst[:, 0:6], in_=z[:, 0:512]), r=[ztk, "bst"], w=["bst"])
                P.op("dve", lambda e: e.bn_stats(out=bst[:, 6:12], in_=z[:, 512:1024]), r=[ztk, "bst"], w=["bst"])
                P.op("dve", lambda e: e.bn_aggr(out=m[:, 0:2], in_=bst[:]), r=["bst", mt], w=[mt])
                _ts(P, "dve", m[:, 2:3], m[:, 1:2], LN_EPS, ALU.add, [mt], [mt])

            def tile2(t, flush=False):
                z = zt[t % NXB]
                ztk = f"xtok{t % NXB}"
                m = mv[t % NXB]
                mt = f"mv{t % NXB}"
                _act(P, m[:, 2:3], m[:, 2:3], AF.Ln, [mt], [mt])
                _act(P, m[:, 3:4], m[:, 2:3], AF.Exp, [mt], [mt], scale=-0.5)
                _ts(P, "dve", z[:], z[:], m[:, 0:1], ALU.subtract, [ztk, mt], [ztk], s2=m[:, 3:4], op1=ALU.mult)
                _tt(P, "dve" if flush else "pool", z[:], z[:], gain_b[:], ALU.mult, [ztk, "gain"], [ztk])
                _tt(P, "pool", z[:], z[:], bias_b[:], ALU.add, [ztk, "lbias"], [ztk])
                _dma(P, "sp", dap(out_d, t * 128 * D, [[D, 128], [1, D]]), z[:], "out", [ztk], [f"out{t}"])
                if t + NXB < NT:
                    load_x(t + NXB)

            its = []
            n = 0
            for g in range(NG):
                for c in range(4):
                    for kt in range(4 * g + 4):
                        its.append((g, c, kt, n))
                        n += 1
            load_qs(0)
            load_sga(0, 0)
            load_sga(0, 1)
            load_mixp(0)
            for t0 in range(NXB):
                load_x(t0)
            pending = []
            deferred = []
            posted = []
            t2_done = set()
            state = {"cool": 0, "idx": 0}
            post_iter = {}

            def do_tile2(t, flush=False):
                if t not in t2_done:
                    tile2(t, flush)
                    t2_done.add(t)

            def push_tile(t):
                for half in range(2):
                    for cc in range(8):
                        pending.append(("mm", t, half, cc))
                    pending.append(("half", t, half))
                pending.append(("fin", t))

            def pop_one(flush=False):
                item = pending.pop(0)
                if item[0] == "mm":
                    tile1_mm(item[1], item[2], item[3])
                    return False
                t = item[1]
                if item[0] == "half":
                    if item[2] == 0 and t >= NXB:
                        do_tile2(t - NXB, flush)
                    tile1_half(t, item[2])
                    state["cool"] = 1
                    return True
                tile1_post(t)
                posted.append(t)
                post_iter[t] = state["idx"]
                if t % 4 == 3 and t // 4 + 2 < NG:
                    load_mixp(t // 4 + 2)
                state["cool"] = 0
                return True

            def flush_groups_upto(gmax):
                while pending and pending[0][1] // 4 <= gmax:
                    pop_one()

            QK(its[0])
            QK(its[1])
            for idx, it in enumerate(its):
                g, c, kt, n = it
                state["idx"] = idx
                if c == 0 and kt == 0 and g + 1 < NG:
                    load_qs(g + 1)
                if idx + 2 < len(its):
                    QK(its[idx + 2])
                last = kt == 4 * g + 3
                if last:
                    flush_groups_upto(g - 2)
                burst = state.pop("burst", 0)
                while burst > 0 and pending:
                    burst -= 1
                    if pop_one():
                        break
                PV(it)
                while deferred and deferred[0][0] <= n:
                    deferred.pop(0)[1]()
                if state["cool"] > 0:
                    state["cool"] -= 1
                elif pending:
                    k = 2 if len(pending) > 22 else 1
                    for _ in range(k):
                        if not pending or pop_one():
                            break
                if last:
                    b = 4 * g + c
                    if b - 4 >= 0:
                        push_tile(b - 4)
                    state["burst"] = 8
                    for t in list(posted):
                        if idx - post_iter[t] >= 3:
                            do_tile2(t)
                    if b == 3 and NG > 1:
                        load_mixp(1)
            while deferred:
                deferred.pop(0)[1]()
            while pending:
                pop_one(flush=True)
            for t in list(posted):
                do_tile2(t, flush=True)
            state["tail"] = True
            for t in range(NT - 4, NT):
                push_tile(t)
                while pending:
                    pop_one(flush=True)
            for t in range(NT):
                do_tile2(t, flush=True)
            _run_block(nc, P, _sem_names(P), g_es, "b")
    return nc


_NC_CACHE = {}


def kernel(x, positions, w_in, pool_w, pool_scale, w_out, ln_gain, ln_bias):
    x = np.asarray(x, dtype=np.float32)
    positions = np.asarray(positions, dtype=np.int32)
    w_in = np.ascontiguousarray(np.asarray(w_in, dtype=np.float32)[0])
    pool_w = np.ascontiguousarray(np.asarray(pool_w, dtype=np.float32)[0])
    pool_scale = np.ascontiguousarray(np.asarray(pool_scale, dtype=np.float32)[0].reshape(4, 128).T)
    w_out = np.ascontiguousarray(np.asarray(w_out, dtype=np.float32)[0])
    ln_gain = np.ascontiguousarray(np.asarray(ln_gain, dtype=np.float32)[0].reshape(1, D))
    ln_bias = np.ascontiguousarray(np.asarray(ln_bias, dtype=np.float32)[0].reshape(1, D))
    if "nc" not in _NC_CACHE:
        _NC_CACHE["nc"] = build_nc()
    nc = _NC_CACHE["nc"]
    in_maps = []
    for b in range(8):
        xb = np.ascontiguousarray(x[b])
        in_maps.append({
            "xT": np.ascontiguousarray(xb.T),
            "x": xb,
            "pos": np.ascontiguousarray(positions[b].reshape(NT, 128).T),
            "w_in": w_in, "pool_w": pool_w, "pool_scale": pool_scale, "w_out": w_out,
            "ln_gain": ln_gain, "ln_bias": ln_bias,
        })
    res = run_bass_kernel_spmd(nc, in_maps, core_ids=list(range(8)))
    if DEBUG:
        kernel.last = res
    return np.stack([np.asarray(r["out"], dtype=np.float32) for r in res.results], axis=0)
```
